# Optimizing a Trainium2 kernel written in Bass

```python
import math
import jax, jax.numpy as jnp
from jax import lax
import numpy as np

D_MODEL = 1024
BATCH = 8
SEQ = 2048
DEPTH = 2

GRID_W = 64
BRANCH_W = 512
N_BRANCH = 3
EPS = 1e-6
SGU_CHUNK = 128
SGU_GROUPS = 4
SGU_GROUP_W = BRANCH_W // SGU_GROUPS
ATT_HEADS = 4
ATT_KV_HEADS = 2
ATT_HEAD_DIM = BRANCH_W // ATT_HEADS
ATT_BLOCK = 128
ROPE_THETA = 10000.0
LSTM_HEADS = 4
LSTM_HEAD_DIM = BRANCH_W // LSTM_HEADS
LSTM_CHUNK = 128
CONV_K = 5
FFN_HIDDEN = -(-8 * D_MODEL // (3 * 256)) * 256
SEG_SIZES = (
    BRANCH_W, BRANCH_W,
    ATT_HEADS * ATT_HEAD_DIM, ATT_KV_HEADS * ATT_HEAD_DIM, ATT_KV_HEADS * ATT_HEAD_DIM,
    BRANCH_W, BRANCH_W, BRANCH_W, BRANCH_W,
    2 * LSTM_HEADS, 2 * LSTM_HEADS,
)
N_IN = sum(SEG_SIZES)

kernel_name = "hybrid_sgu_gqa_mlstm_encoder"


def rmsnorm(x, g):
    xf = x.astype(jnp.float32)
    y = xf * lax.rsqrt(jnp.mean(xf * xf, axis=-1, keepdims=True) + EPS)
    return (y * g.astype(jnp.float32)).astype(x.dtype)


def layernorm(x, g, b):
    xf = x.astype(jnp.float32)
    mu = jnp.mean(xf, axis=-1, keepdims=True)
    xc = xf - mu
    y = xc * lax.rsqrt(jnp.mean(xc * xc, axis=-1, keepdims=True) + EPS)
    return (y * g.astype(jnp.float32) + b.astype(jnp.float32)).astype(x.dtype)


def axial_rope_tables(S):
    rows = S // GRID_W
    row = jnp.repeat(jnp.arange(rows, dtype=jnp.float32), GRID_W)
    col = jnp.tile(jnp.arange(GRID_W, dtype=jnp.float32), rows)
    axis_dim = ATT_HEAD_DIM // 2
    freqs = ROPE_THETA ** (-jnp.arange(axis_dim // 2, dtype=jnp.float32) * 2.0 / axis_dim)
    ang = jnp.concatenate([row[:, None] * freqs[None], col[:, None] * freqs[None]], axis=-1)
    return jnp.cos(ang), jnp.sin(ang)


def apply_rope(x, cos, sin):
    xp = x.reshape(*x.shape[:-1], x.shape[-1] // 2, 2)
    x0, x1 = xp[..., 0], xp[..., 1]
    c = cos[None, :, None, :].astype(x.dtype)
    s = sin[None, :, None, :].astype(x.dtype)
    out = jnp.stack([x0 * c - x1 * s, x0 * s + x1 * c], axis=-1)
    return out.reshape(x.shape)


def sgu_branch(su, sv, ln_g, ln_b, w_s, b_s):
    B_, S, _ = su.shape
    u = jax.nn.gelu(su)
    v = layernorm(jax.nn.gelu(sv), ln_g, ln_b)
    nc = S // SGU_CHUNK
    vr = v.reshape(B_, nc, SGU_CHUNK, SGU_GROUPS, SGU_GROUP_W)
    s = jnp.einsum('gts,bnsgc->bntgc', w_s, vr) + b_s.T[None, None, :, :, None]
    return u * s.reshape(B_, S, BRANCH_W)


def gqa_branch(aq, ak, av, q_norm, k_norm, cos, sin):
    B_, S, _ = aq.shape
    rep = ATT_HEADS // ATT_KV_HEADS
    q = apply_rope(rmsnorm(aq.reshape(B_, S, ATT_HEADS, ATT_HEAD_DIM), q_norm), cos, sin)
    k = apply_rope(rmsnorm(ak.reshape(B_, S, ATT_KV_HEADS, ATT_HEAD_DIM), k_norm), cos, sin)
    v = av.reshape(B_, S, ATT_KV_HEADS, ATT_HEAD_DIM)
    nb = S // ATT_BLOCK
    q = q.reshape(B_, nb, ATT_BLOCK, ATT_KV_HEADS, rep, ATT_HEAD_DIM)
    qb = jnp.moveaxis(q, 1, 0)
    scale = ATT_HEAD_DIM ** -0.5

    def attend(qblk):
        s = jnp.einsum('bqgrd,bkgd->bgrqk', qblk, k).astype(jnp.float32) * scale
        p = jax.nn.softmax(s, axis=-1).astype(v.dtype)
        return jnp.einsum('bgrqk,bkgd->bqgrd', p, v)

    o = lax.map(attend, qb)
    return jnp.moveaxis(o, 0, 1).reshape(B_, S, BRANCH_W)


def centred_depthwise_conv(x, w, b):
    C = x.shape[-1]
    y = lax.conv_general_dilated(x, w[:, None, :], window_strides=(1,),
                                 padding=[(CONV_K // 2, CONV_K // 2)],
                                 dimension_numbers=('NWC', 'WIO', 'NWC'),
                                 feature_group_count=C)
    return y + b


def mlstm_scan(q, k, v, ig, lf):
    B_, H, S, dk = q.shape
    dv = v.shape[-1]
    L = LSTM_CHUNK
    nc = S // L

    def to_chunks(a):
        return jnp.moveaxis(a.reshape(B_, H, nc, L, *a.shape[3:]), 2, 0)

    tri = jnp.tril(jnp.ones((L, L), dtype=bool))

    def step(carry, xs):
        C, n, m = carry
        qc, kc, vc, ic, fc = xs
        b = jnp.cumsum(fc, axis=-1)
        dmat = jnp.where(tri, b[..., :, None] - b[..., None, :] + ic[..., None, :], -jnp.inf)
        m_inter = b + m[..., None]
        m_t = jnp.maximum(m_inter, jnp.max(dmat, axis=-1))
        w_inter = jnp.exp(m_inter - m_t)
        w_intra = jnp.exp(dmat - m_t[..., None])
        s = jnp.einsum('bhtd,bhsd->bhts', qc, kc) * w_intra
        num = w_inter[..., None] * jnp.einsum('bhtd,bhde->bhte', qc, C) + jnp.einsum('bhts,bhse->bhte', s, vc)
        den = w_inter * jnp.einsum('bhtd,bhd->bht', qc, n) + jnp.sum(s, axis=-1)
        h = num / jnp.maximum(jnp.abs(den), jnp.exp(-m_t))[..., None]
        m_new = m_t[..., -1]
        decay = jnp.exp(b[..., -1] + m - m_new)
        wk = jnp.exp(b[..., -1:] - b + ic - m_new[..., None])
        C_new = decay[..., None, None] * C + jnp.einsum('bhs,bhsd,bhse->bhde', wk, kc, vc)
        n_new = decay[..., None] * n + jnp.einsum('bhs,bhsd->bhd', wk, kc)
        return (C_new, n_new, m_new), h

    init = (jnp.zeros((B_, H, dk, dv), jnp.float32),
            jnp.zeros((B_, H, dk), jnp.float32),
            jnp.zeros((B_, H), jnp.float32))
    _, h = lax.scan(step, init, (to_chunks(q), to_chunks(k), to_chunks(v), to_chunks(ig), to_chunks(lf)))
    return jnp.moveaxis(h, 0, 2).reshape(B_, H, S, dv)


def mlstm_branch(lq, lk, lv, lo, li, lf, conv_w, conv_b, igate_b, fgate_b, lstm_norm):
    B_, S, _ = lq.shape
    qk = jax.nn.silu(centred_depthwise_conv(jnp.concatenate([lq, lk], axis=-1), conv_w, conv_b))
    cq, ck = qk[..., :BRANCH_W], qk[..., BRANCH_W:]

    def heads(a):
        return a.reshape(B_, S, LSTM_HEADS, LSTM_HEAD_DIM).transpose(0, 2, 1, 3).astype(jnp.float32)

    q = heads(cq)
    k = heads(ck) * (LSTM_HEAD_DIM ** -0.5)
    v = heads(lv)
    ig = (li.astype(jnp.float32).reshape(B_, S, 2, LSTM_HEADS) + igate_b.astype(jnp.float32)).transpose(2, 0, 3, 1)
    lfg = jax.nn.log_sigmoid(lf.astype(jnp.float32).reshape(B_, S, 2, LSTM_HEADS)
                             + fgate_b.astype(jnp.float32)).transpose(2, 0, 3, 1)
    flip = lambda a: jnp.flip(a, axis=2)
    h_fwd = mlstm_scan(q, k, v, ig[0], lfg[0])
    h_bwd = flip(mlstm_scan(flip(q), flip(k), flip(v), flip(ig[1]), flip(lfg[1])))
    h = (h_fwd + h_bwd).transpose(0, 2, 1, 3)
    h = rmsnorm(h, lstm_norm.reshape(LSTM_HEADS, LSTM_HEAD_DIM))
    return h.reshape(B_, S, BRANCH_W).astype(lq.dtype) * jax.nn.sigmoid(lo)


def hybrid_mixer(xn, cos, sin, w_in, sgu_ln_g, sgu_ln_b, sgu_w, sgu_b, q_norm, k_norm,
                 conv_w, conv_b, igate_b, fgate_b, lstm_norm, w_gate, b_gate, w_branch, w_out):
    B_, S, _ = xn.shape
    points = np.cumsum(SEG_SIZES)[:-1].tolist()
    z = xn @ w_in
    su, sv, aq, ak, av, lq, lk, lv, lo, li, lf = jnp.split(z, points, axis=-1)
    y_a = sgu_branch(su, sv, sgu_ln_g, sgu_ln_b, sgu_w, sgu_b)
    y_b = gqa_branch(aq, ak, av, q_norm, k_norm, cos, sin)
    y_c = mlstm_branch(lq, lk, lv, lo, li, lf, conv_w, conv_b, igate_b, fgate_b, lstm_norm)
    ys = jnp.stack([y_a, y_b, y_c], axis=2)
    proj = jnp.einsum('bsnc,ncd->bsnd', ys, w_branch)
    gates = jax.nn.sigmoid((xn @ w_gate + b_gate).reshape(B_, S, N_BRANCH, D_MODEL))
    return jnp.sum(gates * proj, axis=2) @ w_out


def swiglu(x, w_in, w_out):
    h = x @ w_in
    return (jax.nn.silu(h[..., :FFN_HIDDEN]) * h[..., FFN_HIDDEN:]) @ w_out


def setup_inputs(seed: int = 0) -> dict:
    key = jax.random.key(seed)
    ks = jax.random.split(key, 24)
    f32 = jnp.float32
    nrm = lambda k, shape, scale: jax.random.normal(k, shape, f32) * scale
    gain = lambda k, shape: 1.0 + 0.02 * jax.random.normal(k, shape, f32)
    fbias = jnp.linspace(3.0, 6.0, LSTM_HEADS, dtype=f32)[None, None, :] + nrm(ks[13], (DEPTH, 2, LSTM_HEADS), 0.1)
    return {
        "x": nrm(ks[0], (BATCH, SEQ, D_MODEL), 1.0),
        "norm_mix": gain(ks[1], (DEPTH, D_MODEL)),
        "w_in": nrm(ks[2], (DEPTH, D_MODEL, N_IN), D_MODEL ** -0.5),
        "sgu_ln_g": gain(ks[3], (DEPTH, BRANCH_W)),
        "sgu_ln_b": nrm(ks[4], (DEPTH, BRANCH_W), 0.02),
        "sgu_w": nrm(ks[5], (DEPTH, SGU_GROUPS, SGU_CHUNK, SGU_CHUNK), SGU_CHUNK ** -0.5),
        "sgu_b": gain(ks[6], (DEPTH, SGU_GROUPS, SGU_CHUNK)),
        "q_norm": gain(ks[7], (DEPTH, ATT_HEAD_DIM)),
        "k_norm": gain(ks[8], (DEPTH, ATT_HEAD_DIM)),
        "conv_w": nrm(ks[9], (DEPTH, CONV_K, 2 * BRANCH_W), CONV_K ** -0.5),
        "conv_b": nrm(ks[10], (DEPTH, 2 * BRANCH_W), 0.02),
        "igate_b": nrm(ks[11], (DEPTH, 2, LSTM_HEADS), 0.1),
        "fgate_b": fbias,
        "lstm_norm": gain(ks[12], (DEPTH, BRANCH_W)),
        "w_gate": nrm(ks[14], (DEPTH, D_MODEL, N_BRANCH * D_MODEL), D_MODEL ** -0.5),
        "b_gate": nrm(ks[15], (DEPTH, N_BRANCH * D_MODEL), 0.02),
        "w_branch": nrm(ks[16], (DEPTH, N_BRANCH, BRANCH_W, D_MODEL), BRANCH_W ** -0.5),
        "w_out": nrm(ks[17], (DEPTH, D_MODEL, D_MODEL), D_MODEL ** -0.5),
        "norm_ffn": gain(ks[18], (DEPTH, D_MODEL)),
        "w_ffn_in": nrm(ks[19], (DEPTH, D_MODEL, 2 * FFN_HIDDEN), D_MODEL ** -0.5),
        "w_ffn_out": nrm(ks[20], (DEPTH, FFN_HIDDEN, D_MODEL), FFN_HIDDEN ** -0.5),
    }


def reference(x, norm_mix, w_in, sgu_ln_g, sgu_ln_b, sgu_w, sgu_b, q_norm, k_norm, conv_w, conv_b,
              igate_b, fgate_b, lstm_norm, w_gate, b_gate, w_branch, w_out, norm_ffn, w_ffn_in, w_ffn_out):
    cos, sin = axial_rope_tables(x.shape[1])
    for l in range(DEPTH):
        xn = rmsnorm(x, norm_mix[l])
        x = x + hybrid_mixer(xn, cos, sin, w_in[l], sgu_ln_g[l], sgu_ln_b[l], sgu_w[l], sgu_b[l],
                             q_norm[l], k_norm[l], conv_w[l], conv_b[l], igate_b[l], fgate_b[l],
                             lstm_norm[l], w_gate[l], b_gate[l], w_branch[l], w_out[l])
        x = x + swiglu(rmsnorm(x, norm_ffn[l]), w_ffn_in[l], w_ffn_out[l])
    return x
```

```python
import contextlib
import math
import numpy as np
import concourse.bass as bass
import concourse.mybir as mybir
from concourse.bass_utils import run_bass_kernel_spmd

F32 = mybir.dt.float32
BF16 = mybir.dt.bfloat16
AF = mybir.ActivationFunctionType
ALU = mybir.AluOpType

D = 1024
S = 2048
NB = 8
DEPTH = 2
NT = S // 128
NIN = 4112
FH = 2816
EPS = 1e-6
ENGS = ("pe", "act", "dve", "pool", "sp")

PB_GMIX, PB_GFFN, PB_LNG, PB_LNB, PB_BSB, PB_QN, PB_KN, NPB = 0, 1024, 2048, 2560, 3072, 3584, 3712, 3840
CP_BG, CP_CB, CP_CW, CP_LN, CP_IGB, CP_FGB, CP_GM, CP_GF, CP_QN, CP_KN, CP_LG, CP_LB, NCP = 0, 24, 32, 72, 76, 77, 78, 86, 94, 95, 96, 100, 104
ROWS = [0, 1, 2, 3, 32, 33, 34, 35]


class Prog:
    N_DMA_SEMS = 24

    def __init__(self, nc, stack):
        self.nc = nc
        self.ops = {e: [] for e in ENGS}
        self.count = {e: 0 for e in ENGS}
        self.sem = {}
        for e in ("pe", "act", "dve", "pool"):
            self.sem[e] = stack.enter_context(nc.semaphore("c_" + e))
        self.dsem = [stack.enter_context(nc.semaphore("d%d" % i)) for i in range(self.N_DMA_SEMS)]
        self.dcount = [0] * self.N_DMA_SEMS
        self.dnext = 0
        self.waited = {e: {} for e in ENGS}
        self.bufw = {}
        self.bufr = {}
        self.out_events = []

    def _semh(self, k):
        return self.sem[k] if isinstance(k, str) else self.dsem[k]

    def _need(self, eng, ev, waits, same_ok):
        if ev is None:
            return
        k, val, src = ev
        if src == eng and same_ok:
            return
        if self.waited[eng].get(k, 0) >= val:
            return
        if val > waits.get(k, 0):
            waits[k] = val

    def _deps(self, eng, reads, writes):
        waits = {}
        for k in reads:
            self._need(eng, self.bufw.get(k), waits, eng == "pe")
        for k in writes:
            self._need(eng, self.bufw.get(k), waits, True)
            for ev in self.bufr.get(k, ()):
                self._need(eng, ev, waits, True)
        return waits

    def _emit_waits(self, eng, waits):
        for k, val in waits.items():
            self.waited[eng][k] = val
            h = self._semh(k)
            self.ops[eng].append(lambda e, h=h, val=val: e.wait_ge(h, val))

    def _record(self, ev, reads, writes):
        for k in reads:
            lst = self.bufr.setdefault(k, [])
            for i, o in enumerate(lst):
                if o[0] == ev[0]:
                    lst[i] = ev
                    break
            else:
                lst.append(ev)
        for k in writes:
            self.bufw[k] = ev
            self.bufr[k] = []

    def op(self, eng, fn, reads=(), writes=(), signal=True):
        self._emit_waits(eng, self._deps(eng, reads, writes))
        if signal:
            self.count[eng] += 1
            h = self.sem[eng]
            self.ops[eng].append(lambda e, fn=fn, h=h: fn(e).then_inc(h, 1))
            ev = (eng, self.count[eng], eng)
        else:
            self.ops[eng].append(lambda e, fn=fn: fn(e))
            ev = (eng, self.count[eng] + 1, eng)
        self._record(ev, reads, writes)
        return ev

    def dma(self, eng, out, in_, reads=(), writes=(), is_output=False):
        s = self.dnext
        self.dnext = (self.dnext + 1) % self.N_DMA_SEMS
        waits = self._deps(eng, reads, writes)
        if self.dcount[s] > 0:
            prev = 16 * self.dcount[s]
            if self.waited[eng].get(s, 0) < prev and waits.get(s, 0) < prev:
                waits[s] = prev
        self._emit_waits(eng, waits)
        self.dcount[s] += 1
        h = self.dsem[s]
        self.ops[eng].append(lambda e, h=h, out=out, in_=in_: e.dma_start(out=out, in_=in_).then_inc(h, 16))
        ev = (s, 16 * self.dcount[s], "dma")
        self._record(ev, reads, writes)
        if is_output:
            self.out_events.append(ev)
        return ev

    def barrier(self, engines=("pe", "act", "dve", "sp")):
        for e in engines:
            waits = {}
            for e2 in ("pe", "act", "dve"):
                if e2 != e and self.count[e2] > self.waited[e].get(e2, 0):
                    waits[e2] = self.count[e2]
            self._emit_waits(e, waits)

    def finish(self):
        waits = {}
        for ev in self.out_events:
            self._need("sp", ev, waits, False)
        self._emit_waits("sp", waits)

    def emit(self):
        with self.nc.Block() as block:
            @block.tensor
            def _(e):
                for f in self.ops["pe"]:
                    f(e)

            @block.scalar
            def _(e):
                for f in self.ops["act"]:
                    f(e)

            @block.vector
            def _(e):
                for f in self.ops["dve"]:
                    f(e)

            @block.gpsimd
            def _(e):
                for f in self.ops["pool"]:
                    f(e)

            @block.sync
            def _(e):
                for f in self.ops["sp"]:
                    f(e)


def _dtsize(dt):
    return 4 if dt == F32 else 2


class Arena:
    def __init__(self, big, regions):
        self.big = big
        self.regions = [list(r) for r in regions]

    def get(self, shape, dt, p0=0, p1=128):
        n = 1
        for d_ in shape[1:]:
            n *= d_
        nbytes = (n * _dtsize(dt) + 31) // 32 * 32
        for r in self.regions:
            if r[1] - r[0] >= nbytes:
                off = r[0]
                r[0] += nbytes
                break
        else:
            raise RuntimeError("arena overflow %s %s" % (shape, self.regions))
        ap = self.big[p0:p1, off // 4:(off + n * _dtsize(dt) + 3) // 4]
        if dt != F32:
            ap = ap.bitcast(dt)
            ap = ap[:, 0:n]
        if len(shape) == 3:
            ap = ap.rearrange("p (a b) -> p a b", a=shape[1])
        elif len(shape) == 4:
            ap = ap.rearrange("p (a b c) -> p a b c", a=shape[1], b=shape[2])
        return ap


def build_program(depth=DEPTH, dbg=None):
    dbg = dbg or {}
    taps = dbg.get("taps", ())
    stop_after = dbg.get("stop_after")
    nc = bass.Bass("TRN2", target_bir_lowering=False)
    dt_in = lambda name, shape: nc.dram_tensor(name, list(shape), F32, kind="ExternalInput").ap()
    x_d = dt_in("x", [S, D])
    w_in_d = dt_in("w_in", [DEPTH, D, NIN])
    w_gate_d = dt_in("w_gate", [DEPTH, D, 3 * D])
    w_br_d = dt_in("w_branch", [DEPTH, 3, 512, D])
    w_out_d = dt_in("w_out", [DEPTH, D, D])
    w_f1_d = dt_in("w_ffn_in", [DEPTH, D, 2 * FH])
    w_f2_d = dt_in("w_ffn_out", [DEPTH, FH, D])
    sguw_d = dt_in("sgu_wT", [DEPTH, 128, 4, 128])
    pbc_d = dt_in("pbc", [DEPTH, 128, NPB])
    colp_d = dt_in("colp", [DEPTH, 128, NCP])
    cos_d = dt_in("cos_t", [128, S])
    sin_d = dt_in("sin_t", [128, S])
    pm_d = dt_in("pm_t", [128, 128])
    mask_d = dt_in("mask_t", [128, 2, 128])
    sel_d = dt_in("sel_t", [64, 8, 128])
    identf_d = dt_in("ident_f", [128, 128])
    out_d = nc.dram_tensor("out", [S, D], F32, kind="ExternalOutput").ap()
    tap_d = {}

    with contextlib.ExitStack() as st:
        P = Prog(nc, st)
        x_sb = nc.alloc_sbuf_tensor("x_sb", [128, NT, D], F32)
        xnT = nc.alloc_sbuf_tensor("xnT", [128, 8, S], BF16)
        NW = 6
        wt = [nc.alloc_sbuf_tensor("wt%d" % i, [128, 8, 128], BF16) for i in range(NW)]
        wsT = nc.alloc_sbuf_tensor("wsT", [128, 4, 128], BF16)
        ident_f = nc.alloc_sbuf_tensor("ident_fs", [128, 128], F32)
        ident_b = nc.alloc_sbuf_tensor("ident_b", [128, 128], BF16)
        ones_b = nc.alloc_sbuf_tensor("ones_b", [128, 128], BF16)
        maskT = nc.alloc_sbuf_tensor("maskT", [128, 2, 128], F32)
        colp = nc.alloc_sbuf_tensor("colp_s", [128, NCP], F32)
        zc = nc.alloc_sbuf_tensor("zc", [128, 2], F32)
        YB = 48 * 1024
        AR = int(dbg.get("arena_bytes", 48 * 1024))
        BIG = nc.alloc_sbuf_tensor("big", [128, (YB + AR) // 4], F32)
        ybuf = BIG[:, 0:YB // 4].bitcast(BF16).rearrange("p (n c t) -> p n c t", n=3, c=4)
        ps = [nc.alloc_psum_tensor("ps%d" % i, [128, 512], F32) for i in range(8)]
        psb16 = [p_[:, :].bitcast(BF16) for p_ in ps]

        def mm(out, lhsT, rhs, start, stop, r, w):
            P.op("pe", lambda e: e.matmul(out, lhsT=lhsT, rhs=rhs, start=start, stop=stop), r, w, signal=bool(stop))

        def tr(out, in_, ident, r, w):
            P.op("pe", lambda e: e.transpose(out, in_, ident), r, w)

        def act(out, in_, func, r, w, bias=None, scale=None, accum=None):
            kw = {}
            if bias is not None:
                kw["bias"] = bias
            if scale is not None:
                kw["scale"] = scale
            if accum is not None:
                kw["accum_out"] = accum
            P.op("act", lambda e: e.activation(out=out, in_=in_, func=func, **kw), r, w)

        def tt(out, a, b, op, r, w, eng="dve"):
            P.op(eng, lambda e: e.tensor_tensor(out=out, in0=a, in1=b, op=op), r, w)

        def ts(out, a, s1, s2, op0, op1, r, w, eng="dve"):
            if op1 is None:
                P.op(eng, lambda e: e.tensor_scalar(out=out, in0=a, scalar1=s1, scalar2=None, op0=op0), r, w)
            else:
                P.op(eng, lambda e: e.tensor_scalar(out=out, in0=a, scalar1=s1, scalar2=s2, op0=op0, op1=op1), r, w)

        def stt(out, a, scalar, b, op0, op1, r, w, accum=None):
            if accum is None:
                P.op("dve", lambda e: e.scalar_tensor_tensor(out=out, in0=a, scalar=scalar, in1=b, op0=op0, op1=op1), r, w)
            else:
                P.op("dve", lambda e: e.scalar_tensor_tensor(out=out, in0=a, scalar=scalar, in1=b, op0=op0, op1=op1, accum_out=accum), r, w)

        def cp(out, in_, r, w, eng="dve"):
            if eng == "act":
                act(out, in_, AF.Copy, r, w)
            else:
                P.op(eng, lambda e: e.tensor_copy(out, in_), r, w)

        def memset(ap, val, w, eng="dve"):
            P.op(eng, lambda e: e.memset(ap, val), (), w)

        def recip(out, in_, r, w):
            P.op("dve", lambda e: e.reciprocal(out, in_), r, w)

        wt_next = [0]

        def loadw(src, nchunk=8):
            i = wt_next[0]
            wt_next[0] = (i + 1) % NW
            P.dma("pool", wt[i][:, 0:nchunk, :], src.rearrange("(c p) n -> p c n", p=128), writes=[("wt", i)])
            return i

        def tap(name, ap, shape, reads, dt=F32):
            if name not in taps:
                return
            t = nc.dram_tensor("tap_" + name, list(shape), dt, kind="ExternalOutput").ap()
            tap_d[name] = t
            P.dma("sp", t, ap, reads=reads, is_output=True)

        XK = lambda tts: [("x", t) for t in tts]
        NK = lambda tts: [("xnT", t) for t in tts]
        TG = lambda tg: range(4 * tg, 4 * tg + 4)
        ev_flip = [0]

        def evac(out, in_, r, w):
            ev_flip[0] ^= 1
            cp(out, in_, r, w, eng="act" if ev_flip[0] else "dve")

        P.dma("sp", ident_f[:], identf_d, writes=["ident_f"])
        P.dma("sp", maskT[:], mask_d, writes=["maskT"])
        cp(ident_b[:], ident_f[:], ["ident_f"], ["ident_b"])
        memset(ones_b[:], 1.0, ["ones_b"])
        memset(zc[:], 0.0, ["zc"])
        xv = x_d.rearrange("(t p) d -> p t d", p=128)
        for q in range(4):
            P.dma("sp", x_sb[:, 4 * q:4 * q + 4, :], xv[:, 4 * q:4 * q + 4, :], writes=XK(range(4 * q, 4 * q + 4)))

        def rsqrt_cols(dst, src, mult, r_keys, w_key, tmpa, tmpb):
            ts(tmpa, src, mult, EPS, ALU.mult, ALU.add, r_keys, [w_key + "_a"])
            act(tmpb, tmpa, AF.Sqrt, [w_key + "_a"], [w_key + "_b"])
            recip(dst, tmpb, [w_key + "_b"], [w_key])

        def stage_norm(l, cp_off):
            A = Arena(BIG, [(YB, YB + AR)])
            junk = [A.get([128, D], BF16) for _ in range(2)]
            xs = [A.get([128, D], BF16) for _ in range(3)]
            ss = A.get([128, NT], F32)
            ta = A.get([128, NT], F32)
            tb = A.get([128, NT], F32)
            rstd = A.get([128, NT], F32)
            for t in range(NT):
                if t % 2 == 0:
                    act(junk[0], x_sb[:, t, :], AF.Square, XK([t]), ["njunk0", ("nss", t)], accum=ss[:, t:t + 1])
                else:
                    stt(junk[1], x_sb[:, t, :], 1.0, x_sb[:, t, :], ALU.mult, ALU.mult, XK([t]), ["njunk1", ("nss", t)], accum=ss[:, t:t + 1])
            rsqrt_cols(rstd, ss, 1.0 / D, [("nss", t) for t in range(NT)], "nrstd", ta, tb)
            gcol = colp[:, cp_off:cp_off + 8].unsqueeze(2).to_broadcast([128, 8, 128])
            for t in range(NT):
                xb = xs[t % 3]
                act(xb, x_sb[:, t, :], AF.Copy, XK([t]) + ["nrstd"], [("nxs", t % 3)], scale=rstd[:, t:t + 1])
                pT = psb16[t % 4].rearrange("p (c t) -> p c t", c=8)
                for c in range(8):
                    tr(pT[:, c, :], xb[:, c * 128:(c + 1) * 128], ident_b[:], [("nxs", t % 3), "ident_b"], [("ps", t % 4)])
                tt(xnT[:, :, t * 128:(t + 1) * 128], pT, gcol, ALU.mult, [("ps", t % 4), "colp"], NK([t]))

        def stage_mlstm(l):
            yc = ybuf[:, 2]
            A = Arena(BIG, [(0, 32 * 1024), (YB, YB + AR)])
            Mhl = A.get([64, 2, S], BF16, 0, 64)
            selb = A.get([64, 8, 128], BF16, 0, 64)
            cols = A.get([128, NT, 4, 8], F32)
            decbc = A.get([128, 8, NT], F32)
            mark = [list(r) for r in A.regions]
            lqk = A.get([128, S + 4], BF16)
            cqk = A.get([128, 2, S], BF16)
            ktok = A.get([128, NT, 128], BF16)
            lva = A.get([128, NT, 130], BF16)
            diag = A.get([128, 5, 128], BF16)
            mark2 = [list(r) for r in A.regions]
            memset(lqk[:, 0:2], 0.0, ["lqk"])
            memset(lqk[:, S + 2:S + 4], 0.0, ["lqk"])
            memset(lva[:, :, 128:130], 1.0, ["lva1"])
            pbank = [0]

            def nb01():
                pbank[0] ^= 1
                return pbank[0]

            def setup_parts(h):
                wq = {}
                steps_ = []

                def proj_tg(j, tg):
                    if tg == 0:
                        wq[j] = loadw(w_in_d[l][:, 2048 + j * 512 + h * 128: 2048 + j * 512 + (h + 1) * 128])
                    wi = wq[j]
                    b = nb01()
                    for k in range(8):
                        mm(ps[b][:, :], wt[wi][:, k, :], xnT[:, k, tg * 512:(tg + 1) * 512], k == 0, k == 7, [("wt", wi)] + NK(TG(tg)), [("ps", b)])
                    evac(lqk[:, 2 + tg * 512: 2 + (tg + 1) * 512], ps[b][:, :], [("ps", b)], ["lqk"])

                def diag_(j):
                    ch = j * 4 + h
                    for tap_ in range(5):
                        ts(diag[:, tap_, :], ident_b[:], colp[:, CP_CW + tap_ * 8 + ch: CP_CW + tap_ * 8 + ch + 1], None, ALU.mult, None, ["ident_b", "colp"], ["diag"])

                def conv_tg(j, tg):
                    ch = j * 4 + h
                    b = nb01()
                    for tap_ in range(5):
                        mm(ps[b][:, :], diag[:, tap_, :], lqk[:, tg * 512 + tap_: tg * 512 + tap_ + 512], tap_ == 0, tap_ == 4, ["diag", "lqk"], [("ps", b)])
                    act(cqk[:, j, tg * 512:(tg + 1) * 512], ps[b][:, :], AF.Silu, [("ps", b), "colp"], [("cqk", j)], bias=colp[:, CP_CB + ch: CP_CB + ch + 1])

                def ktok_hf(hf):
                    b = nb01()
                    pT = psb16[b].rearrange("p (c t) -> p c t", c=8)
                    for i in range(8):
                        c = hf * 8 + i
                        tr(pT[:, i, :], cqk[:, 1, c * 128:(c + 1) * 128], ident_b[:], [("cqk", 1), "ident_b"], [("ps", b)])
                    ts(ktok[:, hf * 8:(hf + 1) * 8, :], pT, 128.0 ** -0.5, None, ALU.mult, None, [("ps", b)], ["ktok"])

                def lv_t4(t4):
                    if t4 == 0:
                        wq["v"] = loadw(w_in_d[l][:, 3072 + h * 128: 3072 + (h + 1) * 128])
                    wi = wq["v"]
                    b = nb01()
                    pv = ps[b].rearrange("p (c t) -> p c t", c=4)
                    for i in range(4):
                        t = t4 * 4 + i
                        for k in range(8):
                            mm(pv[:, i, :], xnT[:, k, t * 128:(t + 1) * 128], wt[wi][:, k, :], k == 0, k == 7, [("wt", wi)] + NK([t]), [("ps", b)])
                    evac(lva[:, t4 * 4:t4 * 4 + 4, 0:128], pv, [("ps", b)], ["lva"])

                def path(j):
                    for tg in range(4):
                        steps_.append(lambda j=j, tg=tg: proj_tg(j, tg))
                    steps_.append(lambda j=j: diag_(j))
                    for tg in range(4):
                        steps_.append(lambda j=j, tg=tg: conv_tg(j, tg))
                path(1)
                for hf in range(2):
                    steps_.append(lambda hf=hf: ktok_hf(hf))
                for t4 in range(4):
                    steps_.append(lambda t4=t4: lv_t4(t4))
                path(0)
                return steps_

            class Ticker:
                def __init__(self, steps_):
                    self.steps_ = steps_
                    self.i = 0

                def tick(self, n=1, limit=None):
                    lim = len(self.steps_) if limit is None else min(limit, len(self.steps_))
                    for _ in range(n):
                        if self.i < lim:
                            self.steps_[self.i]()
                            self.i += 1

                def flush(self):
                    self.tick(len(self.steps_))

            tk0 = Ticker(setup_parts(0))
            R2 = A.get([64, S], F32, 0, 64)
            sel = A.get([64, 8, 128], F32, 0, 64)
            R1 = A.get([64, S], F32, 0, 64)
            R3 = A.get([64, S], F32, 0, 64)
            gstrip = A.get([128, 8, 16], F32)
            wgi = A.get([128, 8, 64], BF16)
            wgf = A.get([128, 8, 64], BF16)
            mend = A.get([64, NT], F32, 0, 64)
            mprev = A.get([64, NT], F32, 0, 64)
            dec = A.get([64, NT], F32, 0, 64)
            P.dma("sp", sel, sel_d, writes=["sel"])
            P.dma("sp", gstrip, w_in_d[l][:, 4096:4112].rearrange("(c p) n -> p c n", p=128), writes=["gstrip"])
            memset(wgi, 0.0, ["wgi"])
            memset(wgf, 0.0, ["wgf"])
            cp(wgi[:, :, 0:4], gstrip[:, :, 0:4], ["gstrip"], ["wgi"])
            cp(wgi[:, :, 32:36], gstrip[:, :, 4:8], ["gstrip"], ["wgi"])
            cp(wgf[:, :, 0:4], gstrip[:, :, 8:12], ["gstrip"], ["wgf"])
            cp(wgf[:, :, 32:36], gstrip[:, :, 12:16], ["gstrip"], ["wgf"])
            for tg in range(4):
                sl = slice(tg * 512, (tg + 1) * 512)
                pi, pf = 2 * (tg % 2), 2 * (tg % 2) + 1
                for k in range(8):
                    mm(ps[pi][0:64, :], wgi[:, k, :], xnT[:, k, sl], k == 0, k == 7, ["wgi"] + NK(TG(tg)), [("ps", pi)])
                for k in range(8):
                    mm(ps[pf][0:64, :], wgf[:, k, :], xnT[:, k, sl], k == 0, k == 7, ["wgf"] + NK(TG(tg)), [("ps", pf)])
                act(R1[:, sl], ps[pi][0:64, :], AF.Identity, [("ps", pi), "colp"], ["R1"], bias=colp[0:64, CP_IGB:CP_IGB + 1])
                act(R2[:, sl], ps[pf][0:64, :], AF.Identity, [("ps", pf), "colp"], ["R2"], bias=colp[0:64, CP_FGB:CP_FGB + 1])
            stt(R3, R2, -1.0, R2, ALU.mult, ALU.max, ["R2"], ["R3"])
            tk0.tick()
            act(R3, R3, AF.Exp, ["R3"], ["R3"], scale=-1.0)
            tk0.tick()
            act(R3, R3, AF.Ln, ["R3"], ["R3"], bias=1.0)
            tk0.tick()
            ts(R2, R2, 0.0, None, ALU.min, None, ["R2"], ["R2"])
            tk0.tick()
            tt(R2, R3, R2, ALU.subtract, ["R2", "R3"], ["R2"])
            tk0.tick()
            z0 = zc[0:32, 0:1].to_broadcast([32, S])
            z1 = zc[32:64, 0:1].to_broadcast([32, S])
            P.op("dve", lambda e: e.tensor_tensor_scan(out=R3[0:32, :], data0=z0, data1=R2[0:32, :], initial=0.0, op0=ALU.add, op1=ALU.add), ["R2", "zc", "R3"], ["R3"])
            tk0.tick()
            P.op("dve", lambda e: e.tensor_tensor_scan(out=R3[32:64, ::-1], data0=z1, data1=R2[32:64, ::-1], initial=0.0, op0=ALU.add, op1=ALU.add), ["R2", "zc", "R3"], ["R3"])
            tk0.tick()
            tt(R1, R1, R3, ALU.add, ["R1", "R3"], ["R1"])
            P.op("dve", lambda e: e.tensor_tensor_scan(out=R2[0:32, :], data0=R1[0:32, :], data1=R1[0:32, :], initial=0.0, op0=ALU.max, op1=ALU.max), ["R1", "R2"], ["R2"])
            tk0.tick()
            P.op("dve", lambda e: e.tensor_tensor_scan(out=R2[32:64, ::-1], data0=R1[32:64, ::-1], data1=R1[32:64, ::-1], initial=0.0, op0=ALU.max, op1=ALU.max), ["R1", "R2"], ["R2"])
            tk0.tick()
            tt(R3, R3, R2, ALU.subtract, ["R3", "R2"], ["R3"])
            tk0.tick()
            act(R3, R3, AF.Exp, ["R3"], ["R3"])
            tk0.tick()
            R1v = R1.rearrange("p (c t) -> p c t", c=NT)
            R2v = R2.rearrange("p (c t) -> p c t", c=NT)
            R3v = R3.rearrange("p (c t) -> p c t", c=NT)
            cp(mend[0:32, :], R2v[0:32, :, 127], ["R2"], ["mend"])
            tk0.tick()
            cp(mend[32:64, :], R2v[32:64, :, 0], ["R2"], ["mend"])
            tk0.tick()
            memset(mprev, 0.0, ["mprev"])
            tk0.tick()
            cp(mprev[0:32, 1:NT], mend[0:32, 0:NT - 1], ["mend"], ["mprev"])
            tk0.tick()
            cp(mprev[32:64, 0:NT - 1], mend[32:64, 1:NT], ["mend"], ["mprev"])
            tk0.tick()
            tt(dec, mprev, mend, ALU.subtract, ["mprev", "mend"], ["dec"])
            tk0.tick()
            act(dec, dec, AF.Exp, ["dec"], ["dec"])
            tk0.tick()

            def rows_to_cols(Rsrc, key, q):
                pq = 4 + 2 * (q % 2)
                for c in range(NT):
                    bank = ps[pq + c // 8]
                    tr(bank[:, (c % 8) * 64:(c % 8) * 64 + 64], Rsrc[:, c * 128:(c + 1) * 128], ident_f[0:64, 0:64], [key, "ident_f"], [("ps", pq + c // 8)])
                for hf in range(2):
                    bv = ps[pq + hf].rearrange("p (c r) -> p c r", c=8)
                    cp(cols[:, hf * 8:(hf + 1) * 8, q, 0:4], bv[:, :, 0:4], [("ps", pq + hf)], ["cols"])
                    cp(cols[:, hf * 8:(hf + 1) * 8, q, 4:8], bv[:, :, 32:36], [("ps", pq + hf)], ["cols"])

            rows_to_cols(R1, "R1", 0)
            rows_to_cols(R3, "R3", 1)
            tk0.tick()
            tt(R3v, R1v, mend.unsqueeze(2).to_broadcast([64, NT, 128]), ALU.subtract, ["R1", "mend", "R3"], ["R3"])
            tk0.tick()
            act(R3, R3, AF.Exp, ["R3"], ["R3"])
            tk0.tick()
            rows_to_cols(R3, "R3", 2)
            tk0.tick()
            tt(R1v, mprev.unsqueeze(2).to_broadcast([64, NT, 128]), R2v, ALU.subtract, ["R2", "mprev", "R1"], ["R1"])
            tk0.tick()
            act(R1, R1, AF.Exp, ["R1"], ["R1"])
            tk0.tick()
            rows_to_cols(R1, "R1", 3)
            tk0.tick()
            pdv = ps[0].rearrange("p (r c) -> p r c", r=32)[:, 0:8, :]
            for ri in range(8):
                mm(pdv[:, ri, :], sel[:, ri, :], dec, True, True, ["sel", "dec"], [("ps", 0)])
            cp(decbc, pdv, [("ps", 0)], ["decbc"])
            tk0.tick()
            tap("cols", cols, [128, NT, 4, 8], ["cols"])
            tap("decbc", decbc, [128, 8, NT], ["decbc"])
            tap("Mrow", R2, [64, S], ["R2"])
            cp(selb, sel, ["sel"], ["selb"])
            tk0.tick()
            cp(Mhl[:, 0, :], R2, ["R2"], ["Mhl"])
            tk0.tick()
            tt(Mhl[:, 1, :], R2, Mhl[:, 0, :], ALU.subtract, ["R2", "Mhl"], ["Mhl"])
            tk0.tick()
            tk0.flush()
            P.barrier()
            A.regions = mark2
            hsum = A.get([128, NT, 128], F32)
            KVd = A.get([128, NT, 132], F32)
            hn = KVd[:, :, 0:64].bitcast(BF16)
            Cb = A.get([128, 2, NT + 1, 130], BF16)
            sgl = A.get([128, 512], F32)
            hjunk = A.get([128, 128], BF16)
            ssh = A.get([128, NT], F32)
            tha = A.get([128, NT], F32)
            thb = A.get([128, NT], F32)
            rsh = A.get([128, NT], F32)
            vsb = [A.get([128, 130], BF16) for _ in range(2)]
            T = []
            for s_ in range(2):
                T.append(dict(
                    d2=A.get([128, 2, 128], F32), w=A.get([128, 2, 128], BF16), stm=A.get([128, 2, 128], F32),
                    swT=[A.get([128, 2, 128], BF16) for _ in range(2)], tmpA=A.get([128, 2, 132], F32), tot=A.get([128, 2, 132], F32),
                    den=A.get([128, 2], F32), den2=A.get([128, 2], F32), rden=A.get([128, 2], F32), hh=A.get([128, 2, 128], F32)))
            memset(Cb[:, :, NT, :], 0.0, [("Cb", 0), ("Cb", 1)])
            KVK = [("KVd", c) for c in range(NT)]
            def head_core(h, tkq):
                for d_ in range(2):
                    ri = d_ * 4 + h
                    order = list(range(NT)) if d_ == 0 else list(range(NT - 1, -1, -1))
                    for i, c in enumerate(order):
                        if i % 3 == 0:
                            tkq.tick()
                        v = vsb[i % 2]
                        kv = ("vs", i % 2)
                        act(v[:, 0:129], lva[:, c, 0:129], AF.Copy, ["lva", "lva1", "cols"], [kv], scale=cols[:, c, 2, ri:ri + 1])
                        b = 2 + i % 2
                        mm(ps[b][:, 0:129], ktok[:, c, :], v[:, 0:129], True, True, ["ktok", kv], [("ps", b)])
                        if i == 0:
                            cp(KVd[:, c, 0:129], ps[b][:, 0:129], [("ps", b)], [("KVd", c)])
                        else:
                            cp_ = order[i - 1]
                            stt(KVd[:, c, 0:129], KVd[:, cp_, 0:129], decbc[:, ri, c:c + 1], ps[b][:, 0:129], ALU.mult, ALU.add,
                                [("KVd", cp_), ("ps", b), "decbc"], [("KVd", c)])
                    cp(Cb[:, d_, 0:NT, 0:129], KVd[:, :, 0:129], KVK, [("Cb", d_)], eng="act")
                tkq.flush()
                acolv = lambda c, q: cols[:, c, q, h:h + 5:4]

                def chunk_stages(c, s_, par):
                    t_ = T[s_]
                    swT_ = t_["swT"][par]
                    kW = ("swT", s_, par)
                    pS, pB, pA = ps[2 + s_], ps[4 + s_], ps[6 + s_]
                    kS, kB, kA = ("ps", 2 + s_), ("ps", 4 + s_), ("ps", 6 + s_)
                    cs = slice(c * 128, (c + 1) * 128)
                    K = lambda n: (n, s_)
                    pBv = pB[:, 0:264].rearrange("p (d e) -> p d e", d=2)[:, :, 0:129]

                    def f0():
                        mm(pS[:, 0:128], cqk[:, 1, cs], cqk[:, 0, cs], True, True, [("cqk", 0), ("cqk", 1)], [kS])
                        for d_ in range(2):
                            for hl in range(2):
                                mm(pS[:, 128 + d_ * 128: 256 + d_ * 128], selb[:, d_ * 4 + h, :], Mhl[:, hl, cs], hl == 0, hl == 1, ["selb", "Mhl"], [kS])

                    def f1():
                        tt(t_["d2"], pS[:, 128:384].rearrange("p (d t) -> p d t", d=2), acolv(c, 0).unsqueeze(2).to_broadcast([128, 2, 128]),
                           ALU.subtract, [kS, "cols"], [K("d2")])

                    def f2():
                        act(t_["w"], t_["d2"], AF.Exp, [K("d2")], [K("w")], scale=-1.0)
                        tt(t_["stm"], pS[:, 0:128].unsqueeze(1).to_broadcast([128, 2, 128]), maskT[:, :, :], ALU.mult, [kS, "maskT"], [K("stm")])

                    def f3():
                        stt(swT_, t_["w"], 1.0, t_["stm"], ALU.min, ALU.mult, [K("w"), K("stm")], [kW])

                    def b0():
                        for d_ in range(2):
                            mm(pB[:, d_ * 132: d_ * 132 + 129], swT_[:, d_, :], lva[:, c, 0:129], True, True, [kW, "lva", "lva1"], [kB])
                        for d_ in range(2):
                            cprev = c - 1 if d_ == 0 else c + 1
                            slot = cprev if 0 <= cprev < NT else NT
                            mm(pA[:, d_ * 132: d_ * 132 + 129], cqk[:, 0, cs], Cb[:, d_, slot, 0:129], True, True, [("cqk", 0), ("Cb", d_)], [kA])

                    def b1():
                        for d_ in range(2):
                            act(t_["tmpA"][:, d_, 0:129], pA[:, d_ * 132: d_ * 132 + 129], AF.Copy, [kA, "cols"], [K("tmpA")], scale=cols[:, c, 3, d_ * 4 + h: d_ * 4 + h + 1])

                    def b2():
                        tt(t_["tot"][:, :, 0:129], t_["tmpA"][:, :, 0:129], pBv, ALU.add, [K("tmpA"), kB], [K("tot")])
                        stt(t_["den"], t_["tot"][:, :, 128], -1.0, t_["tot"][:, :, 128], ALU.mult, ALU.max, [K("tot")], [K("den")])
                        tt(t_["den2"], t_["den"], acolv(c, 1), ALU.max, [K("den"), "cols"], [K("den2")])
                        recip(t_["rden"], t_["den2"], [K("den2")], [K("rden")])

                    def b3():
                        for d_ in range(2):
                            act(t_["hh"][:, d_, :], t_["tot"][:, d_, 0:128], AF.Copy, [K("tot"), K("rden")], [K("hh")], scale=t_["rden"][:, d_:d_ + 1])
                        tt(hsum[:, c, :], t_["hh"][:, 0, :], t_["hh"][:, 1, :], ALU.add, [K("hh")], [("hsum", c)], eng="pool")
                    return [f0, f1, f2, f3], [b0, b1, b2, b3]

                NP_ = NT // 2
                stg = [(chunk_stages(i, 0, i % 2), chunk_stages(i + NP_, 1, i % 2)) for i in range(NP_)]

                def emit_part(i, part):
                    (fx, bx), (fy, by) = stg[i]
                    for gx, gy in zip((fx, bx)[part], (fy, by)[part]):
                        gx()
                        gy()

                emit_part(0, 0)
                for i in range(NP_):
                    if i + 1 < NP_:
                        emit_part(i + 1, 0)
                    emit_part(i, 1)
                if h == 0:
                    tap("hsum0", hsum, [128, NT, 128], [("hsum", c) for c in range(NT)])
            def finalize_parts(h):
                st_ = {}

                def f_norm():
                    for c in range(NT):
                        act(hjunk, hsum[:, c, :], AF.Square, [("hsum", c)], ["hjunk", "ssh"], accum=ssh[:, c:c + 1])
                    rsqrt_cols(rsh, ssh, 1.0 / 128, ["ssh"], "rsh", tha, thb)
                    tt(hn, hsum, rsh.unsqueeze(2).to_broadcast([128, NT, 128]), ALU.mult, [("hsum", c) for c in range(NT)] + ["rsh"], ["hn"] + KVK)
                    st_["wi"] = loadw(w_in_d[l][:, 3584 + h * 128: 3584 + (h + 1) * 128])

                def f_tg(tg):
                    wi = st_["wi"]
                    b = 2 + (tg % 2)
                    b2 = 4 + (tg % 2)
                    for k in range(8):
                        mm(ps[b2][:, :], wt[wi][:, k, :], xnT[:, k, tg * 512:(tg + 1) * 512], k == 0, k == 7, [("wt", wi)] + NK(TG(tg)), [("ps", b2)])
                    act(sgl, ps[b2][:, :], AF.Sigmoid, [("ps", b2)], ["sgl"])
                    pT = psb16[b][:, 0:512].rearrange("p (c t) -> p c t", c=4)
                    for i in range(4):
                        tr(pT[:, i, :], hn[:, tg * 4 + i, :], ident_b[:], ["hn", "ident_b"] + KVK, [("ps", b)])
                    stt(yc[:, h, tg * 512:(tg + 1) * 512], psb16[b][:, 0:512], colp[:, CP_LN + h: CP_LN + h + 1], sgl, ALU.mult, ALU.mult,
                        [("ps", b), "sgl", "colp"], [("y", 2, t) for t in TG(tg)])
                return [f_norm, lambda: f_tg(0), lambda: f_tg(1), lambda: f_tg(2), lambda: f_tg(3)]

            tkq = tk0
            for h in range(4):
                head_core(h, tkq)
                fin = finalize_parts(h)
                tkq = Ticker(setup_parts(h + 1) if h + 1 < 4 else [])
                KP = 15
                tkq.tick(3, KP)
                for f_ in fin:
                    f_()
                    tkq.tick(3, KP)
                tkq.tick(KP, KP)
            tap("ycT", yc, [128, 4, S], [("y", 2, t) for t in range(NT)], BF16)

        def stage_sgu(l):
            ya = ybuf[:, 0]
            A = Arena(BIG, [(16 * 1024, 32 * 1024), (YB, YB + AR)])
            vln = A.get([128, NT, 512], BF16)
            pbx = A.get([128, 512], F32)
            Eb = A.get([128, 4, 128], F32)
            tmp = [A.get([128, 512], F32) for _ in range(2)]
            junk = A.get([128, 512], BF16)
            s1 = A.get([128, NT], F32)
            s2 = A.get([128, NT], F32)
            mean = A.get([128, NT], F32)
            var = A.get([128, NT], F32)
            ta = A.get([128, NT], F32)
            tb = A.get([128, NT], F32)
            rstd = A.get([128, NT], F32)
            bsb = pbx
            P.dma("sp", pbx, pbc_d[l][:, PB_BSB:PB_BSB + 512], writes=["pbx"])
            P.dma("pool", wsT[:], sguw_d[l], writes=["wsT"])
            mm(ps[7][:, :], ones_b[:], wsT[:, :, :].rearrange("p g t -> p (g t)"), True, True, ["ones_b", "wsT"], [("ps", 7)])
            for g in range(4):
                stt(Eb[:, g, :], ps[7][:, g * 128:(g + 1) * 128], colp[:, CP_LB + g:CP_LB + g + 1], bsb[:, g * 128:(g + 1) * 128], ALU.mult, ALU.add,
                    [("ps", 7), "colp", "pbx"], ["Eb"])
            pbank = [0]

            def nb01():
                pbank[0] ^= 1
                return pbank[0]
            for c in range(4):
                wi = loadw(w_in_d[l][:, c * 128:(c + 1) * 128])
                for tg in range(4):
                    b = nb01()
                    for k in range(8):
                        mm(ps[b][:, :], wt[wi][:, k, :], xnT[:, k, tg * 512:(tg + 1) * 512], k == 0, k == 7, [("wt", wi)] + NK(TG(tg)), [("ps", b)])
                    act(ya[:, c, tg * 512:(tg + 1) * 512], ps[b][:, :], AF.Gelu_apprx_tanh, [("ps", b)], [("y", 0, t) for t in TG(tg)])
            wis = [loadw(w_in_d[l][:, 512 + c * 128: 512 + (c + 1) * 128]) for c in range(4)]
            for t in range(NT):
                b = 2 + nb01()
                for c in range(4):
                    for k in range(8):
                        mm(ps[b][:, c * 128:(c + 1) * 128], xnT[:, k, t * 128:(t + 1) * 128], wt[wis[c]][:, k, :], k == 0, k == 7, [("wt", wis[c])] + NK([t]), [("ps", b)])
                act(vln[:, t, :], ps[b][:, :], AF.Gelu_apprx_tanh, [("ps", b)], [("vln", t), "s1"], accum=s1[:, t:t + 1])
                act(junk, vln[:, t, :], AF.Square, [("vln", t)], ["sjunk", "s2"], accum=s2[:, t:t + 1])
            ts(mean, s1, 1.0 / 512, None, ALU.mult, None, ["s1"], ["mean"])
            tt(var, mean, mean, ALU.mult, ["mean"], ["var"])
            stt(var, s2, 1.0 / 512, var, ALU.mult, ALU.subtract, ["s2", "var"], ["var"])
            rsqrt_cols(rstd, var, 1.0, ["var"], "srstd", ta, tb)
            for t in range(NT):
                tm = tmp[t % 2]
                kt = ("stmp", t % 2)
                ts(vln[:, t, :], vln[:, t, :], mean[:, t:t + 1], rstd[:, t:t + 1], ALU.subtract, ALU.mult, [("vln", t), "mean", "srstd"], [("vln", t)])
                b = 4 + nb01()
                for g in range(4):
                    mm(ps[b][:, g * 128:(g + 1) * 128], vln[:, t, g * 128:(g + 1) * 128], wsT[:, g, :], True, True, [("vln", t), "wsT"], [("ps", b)])
                for g in range(4):
                    stt(tm[:, g * 128:(g + 1) * 128], ps[b][:, g * 128:(g + 1) * 128], colp[:, CP_LG + g:CP_LG + g + 1], Eb[:, g, :], ALU.mult, ALU.add,
                        [("ps", b), "colp", "Eb"], [kt])
                yav = ya[:, :, t * 128:(t + 1) * 128]
                tt(yav, tm.rearrange("p (g t) -> p g t", g=4), yav, ALU.mult, [kt, ("y", 0, t)], [("y", 0, t)])
            tap("yaT", ya, [128, 4, S], [("y", 0, t) for t in range(NT)], BF16)

        def stage_attn(l):
            yb = ybuf[:, 1]
            A = Arena(BIG, [(YB, YB + AR)])
            kT = A.get([128, 2, S], BF16)
            vatt = A.get([128, NT, 256], BF16)
            cosT = A.get([128, S], F32)
            sinT = A.get([128, S], F32)
            pmf = A.get([128, 128], F32)
            pmb = A.get([128, 128], BF16)
            amark = [list(r) for r in A.regions]
            xsq = [A.get([128, 512], BF16) for _ in range(2)]
            lr = [A.get([128, 512], F32) for _ in range(2)]
            xnb = [A.get([128, 512], BF16) for _ in range(2)]
            t1 = A.get([128, 512], F32)
            t2 = A.get([128, 512], F32)
            P.dma("sp", cosT, cos_d, writes=["cos"])
            P.dma("sp", sinT, sin_d, writes=["sin"])
            P.dma("sp", pmf, pm_d, writes=["pmf"])
            cp(pmb, pmf, ["pmf"], ["pmb"])
            qsteps = [(j, tg) for j in range(6) for tg in range(4)]
            wq = {}
            XB = [0, 1, 2]

            def q_X(n):
                j, tg = qsteps[n]
                if tg == 0:
                    wq[j] = loadw(w_in_d[l][:, 1024 + j * 128: 1024 + (j + 1) * 128])
                wi = wq[j]
                b = XB[n % 3]
                for k in range(8):
                    mm(ps[b][:, :], wt[wi][:, k, :], xnT[:, k, tg * 512:(tg + 1) * 512], k == 0, k == 7, [("wt", wi)] + NK(TG(tg)), [("ps", b)])

            def q_rest(n):
                j, tg = qsteps[n]
                s_ = n % 2
                bX, bS, bP = XB[n % 3], 3 + s_, 5 + s_
                sl = slice(tg * 512, (tg + 1) * 512)
                gcol = colp[:, CP_QN:CP_QN + 1] if j < 4 else colp[:, CP_KN:CP_KN + 1]
                act(xsq[s_], ps[bX][:, :], AF.Square, [("ps", bX)], [("xsq", s_)])
                mm(ps[bS][:, :], ones_b[:], xsq[s_], True, True, ["ones_b", ("xsq", s_)], [("ps", bS)])
                if n + 2 < len(qsteps):
                    q_X(n + 2)
                act(lr[s_], ps[bS][:, :], AF.Ln, [("ps", bS)], [("lr", s_)], bias=EPS, scale=1.0 / 128)
                act(lr[s_], lr[s_], AF.Exp, [("lr", s_)], [("lr", s_)], scale=-0.5)
                stt(xnb[s_], ps[bX][:, :], gcol, lr[s_], ALU.mult, ALU.mult, [("ps", bX), ("lr", s_), "colp"], [("xnb", s_)])
                mm(ps[bP][:, :], pmb, xnb[s_], True, True, ["pmb", ("xnb", s_)], [("ps", bP)])
                tt(t1, xnb[s_], cosT[:, sl], ALU.mult, [("xnb", s_), "cos"], ["t1"])
                tt(t2, ps[bP][:, :], sinT[:, sl], ALU.mult, [("ps", bP), "sin"], ["t2"])
                if j < 4:
                    tt(yb[:, j, sl], t1, t2, ALU.add, ["t1", "t2"], [("y", 1, t) for t in TG(tg)])
                else:
                    tt(kT[:, j - 4, sl], t1, t2, ALU.add, ["t1", "t2"], [("kT", j - 4)])

            q_X(0)
            q_X(1)
            for n in range(len(qsteps)):
                q_rest(n)
            wis = [loadw(w_in_d[l][:, 1792 + c * 128: 1792 + (c + 1) * 128]) for c in range(2)]
            for t in range(NT):
                b = 2 + (t % 2)
                for c in range(2):
                    for k in range(8):
                        mm(ps[b][:, c * 128:(c + 1) * 128], xnT[:, k, t * 128:(t + 1) * 128], wt[wis[c]][:, k, :], k == 0, k == 7, [("wt", wis[c])] + NK([t]), [("ps", b)])
                evac(vatt[:, t, :], ps[b][:, 0:256], [("ps", b)], ["vatt"])
            P.barrier()
            A.regions = amark
            PT = [A.get([128, 512], BF16) for _ in range(3)]
            rrec = A.get([128, 512], F32)
            tap("qT", yb, [128, 4, S], [("y", 1, t) for t in range(NT)], BF16)
            tap("kT", kT, [128, 2, S], [("kT", 0), ("kT", 1)], BF16)
            scale = 128.0 ** -0.5
            asteps = [(h, tg, kc) for h in range(4) for tg in range(4) for kc in range(NT)]
            STB = [0, 1, 6]

            def ST(n):
                h, tg, kc = asteps[n]
                g = h // 2
                b = STB[n % 3]
                mm(ps[b][:, :], kT[:, g, kc * 128:(kc + 1) * 128], yb[:, h, tg * 512:(tg + 1) * 512], True, True,
                   [("kT", g)] + [("y", 1, t) for t in TG(tg)], [("ps", b)])
            ST(0)
            ST(1)
            for n, (h, tg, kc) in enumerate(asteps):
                g = h // 2
                blk = n // NT
                pO, pR = 2 + (blk % 2), 4 + (blk % 2)
                if n + 2 < len(asteps):
                    ST(n + 2)
                b = STB[n % 3]
                pt = PT[n % 3]
                act(pt, ps[b][:, :], AF.Exp, [("ps", b)], [("PT", n % 3)], scale=scale)
                mm(ps[pO][:, :], vatt[:, kc, g * 128:(g + 1) * 128], pt, kc == 0, kc == NT - 1, ["vatt", ("PT", n % 3)], [("ps", pO)])
                mm(ps[pR][:, :], ones_b[:], pt, kc == 0, kc == NT - 1, ["ones_b", ("PT", n % 3)], [("ps", pR)])
                if kc == NT - 1:
                    qs = slice(tg * 512, (tg + 1) * 512)
                    recip(rrec, ps[pR][:, :], [("ps", pR)], ["rrec"])
                    tt(yb[:, h, qs], ps[pO][:, :], rrec, ALU.mult, [("ps", pO), "rrec"], [("y", 1, t) for t in TG(tg)])
            tap("ybT", yb, [128, 4, S], [("y", 1, t) for t in range(NT)], BF16)

        def stage_mix(l):
            A = Arena(BIG, [(YB, YB + AR)])
            mixT = A.get([128, 8, 1024], BF16)
            acc = A.get([128, 1024], F32)
            sg = [A.get([128, 512], F32) for _ in range(2)]
            tm = [A.get([128, 512], F32) for _ in range(2)]
            flip = [0]
            for tb_ in range(2):
                for dc in range(8):
                    for n in range(3):
                        wg = loadw(w_gate_d[l][:, n * 1024 + dc * 128: n * 1024 + (dc + 1) * 128])
                        wb = loadw(w_br_d[l][n][:, dc * 128:(dc + 1) * 128], nchunk=4)
                        for tgi in range(2):
                            tg = tb_ * 2 + tgi
                            f = flip[0]
                            flip[0] ^= 1
                            sl = slice(tg * 512, (tg + 1) * 512)
                            for k in range(8):
                                mm(ps[f][:, :], wt[wg][:, k, :], xnT[:, k, sl], k == 0, k == 7, [("wt", wg)] + NK(TG(tg)), [("ps", f)])
                            act(sg[f], ps[f][:, :], AF.Sigmoid, [("ps", f), "colp"], [("sg", f)], bias=colp[:, CP_BG + n * 8 + dc: CP_BG + n * 8 + dc + 1])
                            for k in range(4):
                                mm(ps[2 + f][:, :], wt[wb][:, k, :], ybuf[:, n, k, sl], k == 0, k == 3, [("wt", wb)] + [("y", n, t) for t in TG(tg)], [("ps", 2 + f)])
                            asl = acc[:, tgi * 512:(tgi + 1) * 512]
                            if n == 0:
                                tt(asl, sg[f], ps[2 + f][:, :], ALU.mult, [("sg", f), ("ps", 2 + f)], [("acc", tgi)])
                            else:
                                tt(tm[f], sg[f], ps[2 + f][:, :], ALU.mult, [("sg", f), ("ps", 2 + f)], [("tm", f)])
                                if n == 1:
                                    tt(asl, asl, tm[f], ALU.add, [("acc", tgi), ("tm", f)], [("acc", tgi)])
                                else:
                                    tt(mixT[:, dc, tgi * 512:(tgi + 1) * 512], asl, tm[f], ALU.add, [("acc", tgi), ("tm", f)], [("mixT", tgi)])
                if tb_ == 0:
                    tap("mixT0", mixT, [128, 8, 1024], [("mixT", 0), ("mixT", 1)], BF16)
                for cg in range(8):
                    wo = loadw(w_out_d[l][:, cg * 128:(cg + 1) * 128])
                    for t4 in range(2):
                        b = 4 + (t4 % 2) + 2 * (cg % 2)
                        pv = ps[b].rearrange("p (c t) -> p c t", c=4)
                        for i in range(4):
                            tl = t4 * 4 + i
                            for k in range(8):
                                mm(pv[:, i, :], mixT[:, k, tl * 128:(tl + 1) * 128], wt[wo][:, k, :], k == 0, k == 7, [("wt", wo), ("mixT", tl // 4)], [("ps", b)])
                        t0 = tb_ * 8 + t4 * 4
                        xv_ = x_sb[:, t0:t0 + 4, cg * 128:(cg + 1) * 128]
                        tt(xv_, xv_, pv, ALU.add, XK(range(t0, t0 + 4)) + [("ps", b)], XK(range(t0, t0 + 4)))

        ov = out_d.rearrange("(t p) d -> p t d", p=128)

        def store_out(q):
            P.dma("sp", ov[:, 4 * q:4 * q + 4, :], x_sb[:, 4 * q:4 * q + 4, :], reads=XK(range(4 * q, 4 * q + 4)), is_output=True)

        def stage_ffn(l):
            A = Arena(BIG, [(0, YB + AR)])
            actT = A.get([128, 22, 1024], BF16)
            sb = [A.get([128, 512], F32) for _ in range(2)]
            flip = [0]
            for tb_ in range(2):
                for j in range(22):
                    w1 = loadw(w_f1_d[l][:, j * 128:(j + 1) * 128])
                    w2 = loadw(w_f1_d[l][:, FH + j * 128: FH + (j + 1) * 128])
                    for tgi in range(2):
                        tg = tb_ * 2 + tgi
                        f = flip[0]
                        flip[0] ^= 1
                        sl = slice(tg * 512, (tg + 1) * 512)
                        for k in range(8):
                            mm(ps[f][:, :], wt[w1][:, k, :], xnT[:, k, sl], k == 0, k == 7, [("wt", w1)] + NK(TG(tg)), [("ps", f)])
                        for k in range(8):
                            mm(ps[2 + f][:, :], wt[w2][:, k, :], xnT[:, k, sl], k == 0, k == 7, [("wt", w2)] + NK(TG(tg)), [("ps", 2 + f)])
                        act(sb[f], ps[f][:, :], AF.Silu, [("ps", f)], [("fs", f)])
                        tt(actT[:, j, tgi * 512:(tgi + 1) * 512], sb[f], ps[2 + f][:, :], ALU.mult, [("fs", f), ("ps", 2 + f)], [("actT", tgi)])
                for cg in range(8):
                    wos = [loadw(w_f2_d[l][r0 * 128:(r0 + n_) * 128, cg * 128:(cg + 1) * 128], nchunk=n_) for (r0, n_) in ((0, 8), (8, 8), (16, 6))]
                    for t4 in range(2):
                        b = 4 + (t4 % 2) + 2 * (cg % 2)
                        pv = ps[b].rearrange("p (c t) -> p c t", c=4)
                        for i in range(4):
                            tl = t4 * 4 + i
                            for j in range(22):
                                wi = wos[j // 8]
                                mm(pv[:, i, :], actT[:, j, tl * 128:(tl + 1) * 128], wt[wi][:, j % 8, :], j == 0, j == 21, [("wt", wi), ("actT", tl // 4)], [("ps", b)])
                        t0 = tb_ * 8 + t4 * 4
                        xv_ = x_sb[:, t0:t0 + 4, cg * 128:(cg + 1) * 128]
                        tt(xv_, xv_, pv, ALU.add, XK(range(t0, t0 + 4)) + [("ps", b)], XK(range(t0, t0 + 4)))
                if l == depth - 1 and not stop_after:
                    store_out(2 * tb_)
                    store_out(2 * tb_ + 1)

        def run():
            for l in range(depth):
                P.dma("sp", colp[:], colp_d[l], writes=["colp"])
                for name, fn in (("norm", lambda: stage_norm(l, CP_GM)), ("mlstm", lambda: stage_mlstm(l)), ("sgu", lambda: stage_sgu(l)),
                                 ("attn", lambda: stage_attn(l)), ("mix", lambda: stage_mix(l)), ("norm2", lambda: stage_norm(l, CP_GF)),
                                 ("ffn", lambda: stage_ffn(l))):
                    if name in dbg.get("skip", ()):
                        continue
                    fn()
                    P.barrier()
                    if name == "norm":
                        tap("xnT", xnT[:], [128, 8, S], NK(range(NT)), BF16)
                    if stop_after == (l, name):
                        return
        run()
        if stop_after or "ffn" in dbg.get("skip", ()):
            for q in range(4):
                store_out(q)
        P.finish()
        P.emit()
    return nc, tap_d


def host_consts():
    rows = S // 64
    row = np.repeat(np.arange(rows, dtype=np.float32), 64)
    col = np.tile(np.arange(64, dtype=np.float32), rows)
    freqs = (np.float32(10000.0) ** (-np.arange(32, dtype=np.float32) * np.float32(2.0) / np.float32(64))).astype(np.float32)
    ang = np.concatenate([row[:, None] * freqs[None], col[:, None] * freqs[None]], axis=-1).astype(np.float32)
    cos = np.repeat(np.cos(ang).astype(np.float32).T, 2, axis=0)
    sin = np.repeat(np.sin(ang).astype(np.float32).T, 2, axis=0)
    pm = np.zeros((128, 128), np.float32)
    for j in range(64):
        pm[2 * j + 1, 2 * j] = -1.0
        pm[2 * j, 2 * j + 1] = 1.0
    sc = np.float32(128.0 ** -0.5)
    s_idx = np.arange(128)[:, None]
    t_idx = np.arange(128)[None, :]
    mask = np.zeros((128, 2, 128), np.float32)
    mask[:, 0, :] = np.where(s_idx <= t_idx, sc, 0.0)
    mask[:, 1, :] = np.where(s_idx >= t_idx, sc, 0.0)
    sel = np.zeros((64, 8, 128), np.float32)
    for ri, r in enumerate(ROWS):
        sel[r, ri, :] = 1.0
    return dict(cos_t=np.ascontiguousarray(cos), sin_t=np.ascontiguousarray(sin), mask_t=mask, sel_t=sel,
                ident_f=np.eye(128, dtype=np.float32), pm_t=pm)


def host_layout(inputs):
    f = lambda k: np.asarray(inputs[k], dtype=np.float32)
    pbc = np.zeros((DEPTH, 128, NPB), np.float32)
    colp = np.zeros((DEPTH, 128, NCP), np.float32)
    for l in range(DEPTH):
        rowv = np.concatenate([f("norm_mix")[l], f("norm_ffn")[l], f("sgu_ln_g")[l], f("sgu_ln_b")[l],
                               f("sgu_b")[l].reshape(-1), f("q_norm")[l], f("k_norm")[l]])
        pbc[l] = np.broadcast_to(rowv[None, :], (128, NPB))
        colp[l, :, CP_BG:CP_BG + 24] = f("b_gate")[l].reshape(24, 128).T
        colp[l, :, CP_CB:CP_CB + 8] = f("conv_b")[l].reshape(8, 128).T
        colp[l, :, CP_CW:CP_CW + 40] = f("conv_w")[l].reshape(5, 8, 128).transpose(2, 0, 1).reshape(128, 40)
        colp[l, :, CP_LN:CP_LN + 4] = f("lstm_norm")[l].reshape(4, 128).T
        colp[l, :, CP_GM:CP_GM + 8] = f("norm_mix")[l].reshape(8, 128).T
        colp[l, :, CP_GF:CP_GF + 8] = f("norm_ffn")[l].reshape(8, 128).T
        colp[l, :, CP_QN] = f("q_norm")[l]
        colp[l, :, CP_KN] = f("k_norm")[l]
        colp[l, :, CP_LG:CP_LG + 4] = f("sgu_ln_g")[l].reshape(4, 128).T
        colp[l, :, CP_LB:CP_LB + 4] = f("sgu_ln_b")[l].reshape(4, 128).T
        for d_ in range(2):
            for h in range(4):
                colp[l, d_ * 32 + h, CP_IGB] = f("igate_b")[l, d_, h]
                colp[l, d_ * 32 + h, CP_FGB] = f("fgate_b")[l, d_, h]
    sgu_wT = np.ascontiguousarray(f("sgu_w").transpose(0, 3, 1, 2))
    shared = dict(w_in=f("w_in"), w_gate=f("w_gate"), w_branch=f("w_branch"), w_out=f("w_out"),
                  w_ffn_in=f("w_ffn_in"), w_ffn_out=f("w_ffn_out"), sgu_wT=sgu_wT, pbc=pbc, colp=colp)
    shared.update(host_consts())
    return shared


_CACHE = {}


def kernel(**inputs):
    shared = host_layout(inputs)
    x = np.asarray(inputs["x"], dtype=np.float32)
    if "nc" not in _CACHE:
        _CACHE["nc"] = build_program()[0]
    nc = _CACHE["nc"]
    in_maps = [dict(shared, x=np.ascontiguousarray(x[b])) for b in range(NB)]
    res = run_bass_kernel_spmd(nc, in_maps, core_ids=list(range(NB)))
    return np.stack([np.asarray(r["out"], dtype=np.float32) for r in res.results], axis=0)
```

```python
import contextlib
import math
import numpy as np
import concourse.bass as bass
import concourse.mybir as mybir
from concourse.bass_utils import run_bass_kernel_spmd

F32 = mybir.dt.float32
BF16 = mybir.dt.bfloat16
AF = mybir.ActivationFunctionType
ALU = mybir.AluOpType

D = 1024
S = 2048
NB = 8
DEPTH = 2
NT = S // 128
NIN = 4112
FH = 2816
EPS = 1e-6
ENGS = ("pe", "act", "dve", "pool", "sp")

PB_GMIX, PB_GFFN, PB_LNG, PB_LNB, PB_BSB, PB_QN, PB_KN, NPB = 0, 1024, 2048, 2560, 3072, 3584, 3712, 3840
CP_BG, CP_CB, CP_CW, CP_LN, CP_IGB, CP_FGB, CP_GM, CP_GF, CP_QN, CP_KN, CP_LG, CP_LB, NCP = 0, 24, 32, 72, 76, 77, 78, 86, 94, 95, 96, 100, 104
ROWS = [0, 1, 2, 3, 32, 33, 34, 35]


class Prog:
    N_DMA_SEMS = 24

    def __init__(self, nc, stack):
        self.nc = nc
        self.ops = {e: [] for e in ENGS}
        self.count = {e: 0 for e in ENGS}
        self.sem = {}
        for e in ("pe", "act", "dve", "pool"):
            self.sem[e] = stack.enter_context(nc.semaphore("c_" + e))
        self.dsem = [stack.enter_context(nc.semaphore("d%d" % i)) for i in range(self.N_DMA_SEMS)]
        self.dcount = [0] * self.N_DMA_SEMS
        self.dnext = 0
        self.waited = {e: {} for e in ENGS}
        self.bufw = {}
        self.bufr = {}
        self.out_events = []

    def _semh(self, k):
        return self.sem[k] if isinstance(k, str) else self.dsem[k]

    def _need(self, eng, ev, waits, same_ok):
        if ev is None:
            return
        k, val, src = ev
        if src == eng and same_ok:
            return
        if self.waited[eng].get(k, 0) >= val:
            return
        if val > waits.get(k, 0):
            waits[k] = val

    def _deps(self, eng, reads, writes):
        waits = {}
        for k in reads:
            self._need(eng, self.bufw.get(k), waits, eng == "pe")
        for k in writes:
            self._need(eng, self.bufw.get(k), waits, True)
            for ev in self.bufr.get(k, ()):
                self._need(eng, ev, waits, True)
        return waits

    def _emit_waits(self, eng, waits):
        for k, val in waits.items():
            self.waited[eng][k] = val
            h = self._semh(k)
            self.ops[eng].append(lambda e, h=h, val=val: e.wait_ge(h, val))

    def _record(self, ev, reads, writes):
        for k in reads:
            lst = self.bufr.setdefault(k, [])
            for i, o in enumerate(lst):
                if o[0] == ev[0]:
                    lst[i] = ev
                    break
            else:
                lst.append(ev)
        for k in writes:
            self.bufw[k] = ev
            self.bufr[k] = []

    def op(self, eng, fn, reads=(), writes=(), signal=True):
        self._emit_waits(eng, self._deps(eng, reads, writes))
        if signal:
            self.count[eng] += 1
            h = self.sem[eng]
            self.ops[eng].append(lambda e, fn=fn, h=h: fn(e).then_inc(h, 1))
            ev = (eng, self.count[eng], eng)
        else:
            self.ops[eng].append(lambda e, fn=fn: fn(e))
            ev = (eng, self.count[eng] + 1, eng)
        self._record(ev, reads, writes)
        return ev

    def dma(self, eng, out, in_, reads=(), writes=(), is_output=False):
        s = self.dnext
        self.dnext = (self.dnext + 1) % self.N_DMA_SEMS
        waits = self._deps(eng, reads, writes)
        if self.dcount[s] > 0:
            prev = 16 * self.dcount[s]
            if self.waited[eng].get(s, 0) < prev and waits.get(s, 0) < prev:
                waits[s] = prev
        self._emit_waits(eng, waits)
        self.dcount[s] += 1
        h = self.dsem[s]
        self.ops[eng].append(lambda e, h=h, out=out, in_=in_: e.dma_start(out=out, in_=in_).then_inc(h, 16))
        ev = (s, 16 * self.dcount[s], "dma")
        self._record(ev, reads, writes)
        if is_output:
            self.out_events.append(ev)
        return ev

    def barrier(self, engines=("pe", "act", "dve", "sp")):
        for e in engines:
            waits = {}
            for e2 in ("pe", "act", "dve"):
                if e2 != e and self.count[e2] > self.waited[e].get(e2, 0):
                    waits[e2] = self.count[e2]
            self._emit_waits(e, waits)

    def finish(self):
        waits = {}
        for ev in self.out_events:
            self._need("sp", ev, waits, False)
        self._emit_waits("sp", waits)

    def emit(self):
        with self.nc.Block() as block:
            @block.tensor
            def _(e):
                for f in self.ops["pe"]:
                    f(e)

            @block.scalar
            def _(e):
                for f in self.ops["act"]:
                    f(e)

            @block.vector
            def _(e):
                for f in self.ops["dve"]:
                    f(e)

            @block.gpsimd
            def _(e):
                for f in self.ops["pool"]:
                    f(e)

            @block.sync
            def _(e):
                for f in self.ops["sp"]:
                    f(e)


def _dtsize(dt):
    return 4 if dt == F32 else 2


class Arena:
    def __init__(self, big, regions):
        self.big = big
        self.regions = [list(r) for r in regions]

    def get(self, shape, dt, p0=0, p1=128):
        n = 1
        for d_ in shape[1:]:
            n *= d_
        nbytes = (n * _dtsize(dt) + 31) // 32 * 32
        for r in self.regions:
            if r[1] - r[0] >= nbytes:
                off = r[0]
                r[0] += nbytes
                break
        else:
            raise RuntimeError("arena overflow %s %s" % (shape, self.regions))
        ap = self.big[p0:p1, off // 4:(off + n * _dtsize(dt) + 3) // 4]
        if dt != F32:
            ap = ap.bitcast(dt)
            ap = ap[:, 0:n]
        if len(shape) == 3:
            ap = ap.rearrange("p (a b) -> p a b", a=shape[1])
        elif len(shape) == 4:
            ap = ap.rearrange("p (a b c) -> p a b c", a=shape[1], b=shape[2])
        return ap


def build_program(depth=DEPTH, dbg=None):
    dbg = dbg or {}
    taps = dbg.get("taps", ())
    stop_after = dbg.get("stop_after")
    nc = bass.Bass("TRN2", target_bir_lowering=False)
    dt_in = lambda name, shape: nc.dram_tensor(name, list(shape), F32, kind="ExternalInput").ap()
    x_d = dt_in("x", [S, D])
    w_in_d = dt_in("w_in", [DEPTH, D, NIN])
    w_gate_d = dt_in("w_gate", [DEPTH, D, 3 * D])
    w_br_d = dt_in("w_branch", [DEPTH, 3, 512, D])
    w_out_d = dt_in("w_out", [DEPTH, D, D])
    w_f1_d = dt_in("w_ffn_in", [DEPTH, D, 2 * FH])
    w_f2_d = dt_in("w_ffn_out", [DEPTH, FH, D])
    sguw_d = dt_in("sgu_wT", [DEPTH, 128, 4, 128])
    pbc_d = dt_in("pbc", [DEPTH, 128, NPB])
    colp_d = dt_in("colp", [DEPTH, 128, NCP])
    cos_d = dt_in("cos_t", [128, S])
    sin_d = dt_in("sin_t", [128, S])
    pm_d = dt_in("pm_t", [128, 128])
    mask_d = dt_in("mask_t", [128, 2, 128])
    sel_d = dt_in("sel_t", [64, 8, 128])
    identf_d = dt_in("ident_f", [128, 128])
    out_d = nc.dram_tensor("out", [S, D], F32, kind="ExternalOutput").ap()
    tap_d = {}

    with contextlib.ExitStack() as st:
        P = Prog(nc, st)
        x_sb = nc.alloc_sbuf_tensor("x_sb", [128, NT, D], F32)
        xnT = nc.alloc_sbuf_tensor("xnT", [128, 8, S], BF16)
        NW = 6
        wt = [nc.alloc_sbuf_tensor("wt%d" % i, [128, 8, 128], BF16) for i in range(NW)]
        wsT = nc.alloc_sbuf_tensor("wsT", [128, 4, 128], BF16)
        ident_f = nc.alloc_sbuf_tensor("ident_fs", [128, 128], F32)
        ident_b = nc.alloc_sbuf_tensor("ident_b", [128, 128], BF16)
        ones_b = nc.alloc_sbuf_tensor("ones_b", [128, 128], BF16)
        maskT = nc.alloc_sbuf_tensor("maskT", [128, 2, 128], F32)
        colp = nc.alloc_sbuf_tensor("colp_s", [128, NCP], F32)
        zc = nc.alloc_sbuf_tensor("zc", [128, 2], F32)
        YB = 48 * 1024
        AR = int(dbg.get("arena_bytes", 48 * 1024))
        BIG = nc.alloc_sbuf_tensor("big", [128, (YB + AR) // 4], F32)
        ybuf = BIG[:, 0:YB // 4].bitcast(BF16).rearrange("p (n c t) -> p n c t", n=3, c=4)
        ps = [nc.alloc_psum_tensor("ps%d" % i, [128, 512], F32) for i in range(8)]
        psb16 = [p_[:, :].bitcast(BF16) for p_ in ps]

        def mm(out, lhsT, rhs, start, stop, r, w):
            P.op("pe", lambda e: e.matmul(out, lhsT=lhsT, rhs=rhs, start=start, stop=stop), r, w, signal=bool(stop))

        def tr(out, in_, ident, r, w):
            P.op("pe", lambda e: e.transpose(out, in_, ident), r, w)

        def act(out, in_, func, r, w, bias=None, scale=None, accum=None):
            kw = {}
            if bias is not None:
                kw["bias"] = bias
            if scale is not None:
                kw["scale"] = scale
            if accum is not None:
                kw["accum_out"] = accum
            P.op("act", lambda e: e.activation(out=out, in_=in_, func=func, **kw), r, w)

        def tt(out, a, b, op, r, w, eng="dve"):
            P.op(eng, lambda e: e.tensor_tensor(out=out, in0=a, in1=b, op=op), r, w)

        def ts(out, a, s1, s2, op0, op1, r, w, eng="dve"):
            if op1 is None:
                P.op(eng, lambda e: e.tensor_scalar(out=out, in0=a, scalar1=s1, scalar2=None, op0=op0), r, w)
            else:
                P.op(eng, lambda e: e.tensor_scalar(out=out, in0=a, scalar1=s1, scalar2=s2, op0=op0, op1=op1), r, w)

        def stt(out, a, scalar, b, op0, op1, r, w, accum=None):
            if accum is None:
                P.op("dve", lambda e: e.scalar_tensor_tensor(out=out, in0=a, scalar=scalar, in1=b, op0=op0, op1=op1), r, w)
            else:
                P.op("dve", lambda e: e.scalar_tensor_tensor(out=out, in0=a, scalar=scalar, in1=b, op0=op0, op1=op1, accum_out=accum), r, w)

        def cp(out, in_, r, w, eng="dve"):
            if eng == "act":
                act(out, in_, AF.Copy, r, w)
            else:
                P.op(eng, lambda e: e.tensor_copy(out, in_), r, w)

        def memset(ap, val, w, eng="dve"):
            P.op(eng, lambda e: e.memset(ap, val), (), w)

        def recip(out, in_, r, w):
            P.op("dve", lambda e: e.reciprocal(out, in_), r, w)

        wt_next = [0]

        def loadw(src, nchunk=8):
            i = wt_next[0]
            wt_next[0] = (i + 1) % NW
            P.dma("pool", wt[i][:, 0:nchunk, :], src.rearrange("(c p) n -> p c n", p=128), writes=[("wt", i)])
            return i

        def tap(name, ap, shape, reads, dt=F32):
            if name not in taps:
                return
            t = nc.dram_tensor("tap_" + name, list(shape), dt, kind="ExternalOutput").ap()
            tap_d[name] = t
            P.dma("sp", t, ap, reads=reads, is_output=True)

        XK = lambda tts: [("x", t) for t in tts]
        NK = lambda tts: [("xnT", t) for t in tts]
        TG = lambda tg: range(4 * tg, 4 * tg + 4)
        ev_flip = [0]

        def evac(out, in_, r, w):
            ev_flip[0] ^= 1
            cp(out, in_, r, w, eng="act" if ev_flip[0] else "dve")

        P.dma("sp", ident_f[:], identf_d, writes=["ident_f"])
        P.dma("sp", maskT[:], mask_d, writes=["maskT"])
        cp(ident_b[:], ident_f[:], ["ident_f"], ["ident_b"])
        memset(ones_b[:], 1.0, ["ones_b"])
        memset(zc[:], 0.0, ["zc"])
        xv = x_d.rearrange("(t p) d -> p t d", p=128)
        for q in range(4):
            P.dma("sp", x_sb[:, 4 * q:4 * q + 4, :], xv[:, 4 * q:4 * q + 4, :], writes=XK(range(4 * q, 4 * q + 4)))

        def rsqrt_cols(dst, src, mult, r_keys, w_key, tmpa, tmpb):
            ts(tmpa, src, mult, EPS, ALU.mult, ALU.add, r_keys, [w_key + "_a"])
            act(tmpb, tmpa, AF.Sqrt, [w_key + "_a"], [w_key + "_b"])
            recip(dst, tmpb, [w_key + "_b"], [w_key])

        def stage_norm(l, cp_off):
            A = Arena(BIG, [(YB, YB + AR)])
            junk = [A.get([128, D], BF16) for _ in range(2)]
            xs = [A.get([128, D], BF16) for _ in range(3)]
            ss = A.get([128, NT], F32)
            ta = A.get([128, NT], F32)
            tb = A.get([128, NT], F32)
            rstd = A.get([128, NT], F32)
            for t in range(NT):
                if t % 2 == 0:
                    act(junk[0], x_sb[:, t, :], AF.Square, XK([t]), ["njunk0", ("nss", t)], accum=ss[:, t:t + 1])
                else:
                    stt(junk[1], x_sb[:, t, :], 1.0, x_sb[:, t, :], ALU.mult, ALU.mult, XK([t]), ["njunk1", ("nss", t)], accum=ss[:, t:t + 1])
            rsqrt_cols(rstd, ss, 1.0 / D, [("nss", t) for t in range(NT)], "nrstd", ta, tb)
            gcol = colp[:, cp_off:cp_off + 8].unsqueeze(2).to_broadcast([128, 8, 128])
            for t in range(NT):
                xb = xs[t % 3]
                act(xb, x_sb[:, t, :], AF.Copy, XK([t]) + ["nrstd"], [("nxs", t % 3)], scale=rstd[:, t:t + 1])
                pT = psb16[t % 4].rearrange("p (c t) -> p c t", c=8)
                for c in range(8):
                    tr(pT[:, c, :], xb[:, c * 128:(c + 1) * 128], ident_b[:], [("nxs", t % 3), "ident_b"], [("ps", t % 4)])
                tt(xnT[:, :, t * 128:(t + 1) * 128], pT, gcol, ALU.mult, [("ps", t % 4), "colp"], NK([t]))

        def stage_mlstm(l):
            yc = ybuf[:, 2]
            A = Arena(BIG, [(0, 32 * 1024), (YB, YB + AR)])
            Mhl = A.get([64, 2, S], BF16, 0, 64)
            selb = A.get([64, 8, 128], BF16, 0, 64)
            cols = A.get([128, NT, 4, 8], F32)
            decbc = A.get([128, 8, NT], F32)
            mark = [list(r) for r in A.regions]
            lqk = A.get([128, S + 4], BF16)
            cqk = A.get([128, 2, S], BF16)
            ktok = A.get([128, NT, 128], BF16)
            lva = A.get([128, NT, 130], BF16)
            diag = A.get([128, 5, 128], BF16)
            mark2 = [list(r) for r in A.regions]
            memset(lqk[:, 0:2], 0.0, ["lqk"])
            memset(lqk[:, S + 2:S + 4], 0.0, ["lqk"])
            memset(lva[:, :, 128:130], 1.0, ["lva1"])
            pbank = [0]

            def nb01():
                pbank[0] ^= 1
                return pbank[0]

            def setup_parts(h):
                wq = {}
                steps_ = []

                def proj_tg(j, tg):
                    if tg == 0:
                        wq[j] = loadw(w_in_d[l][:, 2048 + j * 512 + h * 128: 2048 + j * 512 + (h + 1) * 128])
                    wi = wq[j]
                    b = nb01()
                    for k in range(8):
                        mm(ps[b][:, :], wt[wi][:, k, :], xnT[:, k, tg * 512:(tg + 1) * 512], k == 0, k == 7, [("wt", wi)] + NK(TG(tg)), [("ps", b)])
                    evac(lqk[:, 2 + tg * 512: 2 + (tg + 1) * 512], ps[b][:, :], [("ps", b)], ["lqk"])

                def diag_(j):
                    ch = j * 4 + h
                    for tap_ in range(5):
                        ts(diag[:, tap_, :], ident_b[:], colp[:, CP_CW + tap_ * 8 + ch: CP_CW + tap_ * 8 + ch + 1], None, ALU.mult, None, ["ident_b", "colp"], ["diag"])

                def conv_tg(j, tg):
                    ch = j * 4 + h
                    b = nb01()
                    for tap_ in range(5):
                        mm(ps[b][:, :], diag[:, tap_, :], lqk[:, tg * 512 + tap_: tg * 512 + tap_ + 512], tap_ == 0, tap_ == 4, ["diag", "lqk"], [("ps", b)])
                    act(cqk[:, j, tg * 512:(tg + 1) * 512], ps[b][:, :], AF.Silu, [("ps", b), "colp"], [("cqk", j)], bias=colp[:, CP_CB + ch: CP_CB + ch + 1])

                def ktok_hf(hf):
                    b = nb01()
                    pT = psb16[b].rearrange("p (c t) -> p c t", c=8)
                    for i in range(8):
                        c = hf * 8 + i
                        tr(pT[:, i, :], cqk[:, 1, c * 128:(c + 1) * 128], ident_b[:], [("cqk", 1), "ident_b"], [("ps", b)])
                    ts(ktok[:, hf * 8:(hf + 1) * 8, :], pT, 128.0 ** -0.5, None, ALU.mult, None, [("ps", b)], ["ktok"])

                def lv_t4(t4):
                    if t4 == 0:
                        wq["v"] = loadw(w_in_d[l][:, 3072 + h * 128: 3072 + (h + 1) * 128])
                    wi = wq["v"]
                    b = nb01()
                    pv = ps[b].rearrange("p (c t) -> p c t", c=4)
                    for i in range(4):
                        t = t4 * 4 + i
                        for k in range(8):
                            mm(pv[:, i, :], xnT[:, k, t * 128:(t + 1) * 128], wt[wi][:, k, :], k == 0, k == 7, [("wt", wi)] + NK([t]), [("ps", b)])
                    evac(lva[:, t4 * 4:t4 * 4 + 4, 0:128], pv, [("ps", b)], ["lva"])

                for j in range(2):
                    for tg in range(4):
                        steps_.append(lambda j=j, tg=tg: proj_tg(j, tg))
                    steps_.append(lambda j=j: diag_(j))
                    for tg in range(4):
                        steps_.append(lambda j=j, tg=tg: conv_tg(j, tg))
                for hf in range(2):
                    steps_.append(lambda hf=hf: ktok_hf(hf))
                for t4 in range(4):
                    steps_.append(lambda t4=t4: lv_t4(t4))
                return steps_

            class Ticker:
                def __init__(self, steps_):
                    self.steps_ = steps_
                    self.i = 0

                def tick(self, n=1):
                    for _ in range(n):
                        if self.i < len(self.steps_):
                            self.steps_[self.i]()
                            self.i += 1

                def flush(self):
                    self.tick(len(self.steps_))

            tk0 = Ticker(setup_parts(0))
            R2 = A.get([64, S], F32, 0, 64)
            sel = A.get([64, 8, 128], F32, 0, 64)
            R1 = A.get([64, S], F32, 0, 64)
            R3 = A.get([64, S], F32, 0, 64)
            gstrip = A.get([128, 8, 16], F32)
            wgi = A.get([128, 8, 64], BF16)
            wgf = A.get([128, 8, 64], BF16)
            mend = A.get([64, NT], F32, 0, 64)
            mprev = A.get([64, NT], F32, 0, 64)
            dec = A.get([64, NT], F32, 0, 64)
            P.dma("sp", sel, sel_d, writes=["sel"])
            P.dma("sp", gstrip, w_in_d[l][:, 4096:4112].rearrange("(c p) n -> p c n", p=128), writes=["gstrip"])
            memset(wgi, 0.0, ["wgi"])
            memset(wgf, 0.0, ["wgf"])
            cp(wgi[:, :, 0:4], gstrip[:, :, 0:4], ["gstrip"], ["wgi"])
            cp(wgi[:, :, 32:36], gstrip[:, :, 4:8], ["gstrip"], ["wgi"])
            cp(wgf[:, :, 0:4], gstrip[:, :, 8:12], ["gstrip"], ["wgf"])
            cp(wgf[:, :, 32:36], gstrip[:, :, 12:16], ["gstrip"], ["wgf"])
            for tg in range(4):
                sl = slice(tg * 512, (tg + 1) * 512)
                pi, pf = 2 * (tg % 2), 2 * (tg % 2) + 1
                for k in range(8):
                    mm(ps[pi][0:64, :], wgi[:, k, :], xnT[:, k, sl], k == 0, k == 7, ["wgi"] + NK(TG(tg)), [("ps", pi)])
                for k in range(8):
                    mm(ps[pf][0:64, :], wgf[:, k, :], xnT[:, k, sl], k == 0, k == 7, ["wgf"] + NK(TG(tg)), [("ps", pf)])
                act(R1[:, sl], ps[pi][0:64, :], AF.Identity, [("ps", pi), "colp"], ["R1"], bias=colp[0:64, CP_IGB:CP_IGB + 1])
                act(R2[:, sl], ps[pf][0:64, :], AF.Identity, [("ps", pf), "colp"], ["R2"], bias=colp[0:64, CP_FGB:CP_FGB + 1])
            stt(R3, R2, -1.0, R2, ALU.mult, ALU.max, ["R2"], ["R3"])
            tk0.tick()
            act(R3, R3, AF.Exp, ["R3"], ["R3"], scale=-1.0)
            tk0.tick()
            act(R3, R3, AF.Ln, ["R3"], ["R3"], bias=1.0)
            tk0.tick()
            ts(R2, R2, 0.0, None, ALU.min, None, ["R2"], ["R2"])
            tk0.tick()
            tt(R2, R3, R2, ALU.subtract, ["R2", "R3"], ["R2"])
            tk0.tick()
            z0 = zc[0:32, 0:1].to_broadcast([32, S])
            z1 = zc[32:64, 0:1].to_broadcast([32, S])
            P.op("dve", lambda e: e.tensor_tensor_scan(out=R3[0:32, :], data0=z0, data1=R2[0:32, :], initial=0.0, op0=ALU.add, op1=ALU.add), ["R2", "zc", "R3"], ["R3"])
            tk0.tick()
            P.op("dve", lambda e: e.tensor_tensor_scan(out=R3[32:64, ::-1], data0=z1, data1=R2[32:64, ::-1], initial=0.0, op0=ALU.add, op1=ALU.add), ["R2", "zc", "R3"], ["R3"])
            tk0.tick()
            tt(R1, R1, R3, ALU.add, ["R1", "R3"], ["R1"])
            P.op("dve", lambda e: e.tensor_tensor_scan(out=R2[0:32, :], data0=R1[0:32, :], data1=R1[0:32, :], initial=0.0, op0=ALU.max, op1=ALU.max), ["R1", "R2"], ["R2"])
            tk0.tick()
            P.op("dve", lambda e: e.tensor_tensor_scan(out=R2[32:64, ::-1], data0=R1[32:64, ::-1], data1=R1[32:64, ::-1], initial=0.0, op0=ALU.max, op1=ALU.max), ["R1", "R2"], ["R2"])
            tk0.tick()
            tt(R3, R3, R2, ALU.subtract, ["R3", "R2"], ["R3"])
            tk0.tick()
            act(R3, R3, AF.Exp, ["R3"], ["R3"])
            tk0.tick()
            R1v = R1.rearrange("p (c t) -> p c t", c=NT)
            R2v = R2.rearrange("p (c t) -> p c t", c=NT)
            R3v = R3.rearrange("p (c t) -> p c t", c=NT)
            cp(mend[0:32, :], R2v[0:32, :, 127], ["R2"], ["mend"])
            tk0.tick()
            cp(mend[32:64, :], R2v[32:64, :, 0], ["R2"], ["mend"])
            tk0.tick()
            memset(mprev, 0.0, ["mprev"])
            tk0.tick()
            cp(mprev[0:32, 1:NT], mend[0:32, 0:NT - 1], ["mend"], ["mprev"])
            tk0.tick()
            cp(mprev[32:64, 0:NT - 1], mend[32:64, 1:NT], ["mend"], ["mprev"])
            tk0.tick()
            tt(dec, mprev, mend, ALU.subtract, ["mprev", "mend"], ["dec"])
            tk0.tick()
            act(dec, dec, AF.Exp, ["dec"], ["dec"])
            tk0.tick()

            def rows_to_cols(Rsrc, key, q):
                pq = 4 + 2 * (q % 2)
                for c in range(NT):
                    bank = ps[pq + c // 8]
                    tr(bank[:, (c % 8) * 64:(c % 8) * 64 + 64], Rsrc[:, c * 128:(c + 1) * 128], ident_f[0:64, 0:64], [key, "ident_f"], [("ps", pq + c // 8)])
                for hf in range(2):
                    bv = ps[pq + hf].rearrange("p (c r) -> p c r", c=8)
                    cp(cols[:, hf * 8:(hf + 1) * 8, q, 0:4], bv[:, :, 0:4], [("ps", pq + hf)], ["cols"])
                    cp(cols[:, hf * 8:(hf + 1) * 8, q, 4:8], bv[:, :, 32:36], [("ps", pq + hf)], ["cols"])

            rows_to_cols(R1, "R1", 0)
            rows_to_cols(R3, "R3", 1)
            tk0.tick()
            tt(R3v, R1v, mend.unsqueeze(2).to_broadcast([64, NT, 128]), ALU.subtract, ["R1", "mend", "R3"], ["R3"])
            tk0.tick()
            act(R3, R3, AF.Exp, ["R3"], ["R3"])
            tk0.tick()
            rows_to_cols(R3, "R3", 2)
            tk0.tick()
            tt(R1v, mprev.unsqueeze(2).to_broadcast([64, NT, 128]), R2v, ALU.subtract, ["R2", "mprev", "R1"], ["R1"])
            tk0.tick()
            act(R1, R1, AF.Exp, ["R1"], ["R1"])
            tk0.tick()
            rows_to_cols(R1, "R1", 3)
            tk0.tick()
            pdv = ps[0].rearrange("p (r c) -> p r c", r=32)[:, 0:8, :]
            for ri in range(8):
                mm(pdv[:, ri, :], sel[:, ri, :], dec, True, True, ["sel", "dec"], [("ps", 0)])
            cp(decbc, pdv, [("ps", 0)], ["decbc"])
            tk0.tick()
            tap("cols", cols, [128, NT, 4, 8], ["cols"])
            tap("decbc", decbc, [128, 8, NT], ["decbc"])
            tap("Mrow", R2, [64, S], ["R2"])
            cp(selb, sel, ["sel"], ["selb"])
            tk0.tick()
            cp(Mhl[:, 0, :], R2, ["R2"], ["Mhl"])
            tk0.tick()
            tt(Mhl[:, 1, :], R2, Mhl[:, 0, :], ALU.subtract, ["R2", "Mhl"], ["Mhl"])
            tk0.tick()
            tk0.flush()
            P.barrier()
            A.regions = mark2
            hsum = A.get([128, NT, 128], F32)
            KVd = A.get([128, NT, 132], F32)
            hn = KVd[:, :, 0:64].bitcast(BF16)
            Cb = A.get([128, 2, NT + 1, 130], BF16)
            sgl = A.get([128, 512], F32)
            hjunk = A.get([128, 128], BF16)
            ssh = A.get([128, NT], F32)
            tha = A.get([128, NT], F32)
            thb = A.get([128, NT], F32)
            rsh = A.get([128, NT], F32)
            vsb = [A.get([128, 130], BF16) for _ in range(2)]
            T = []
            for s_ in range(2):
                T.append(dict(
                    d2=A.get([128, 2, 128], F32), w=A.get([128, 2, 128], BF16), stm=A.get([128, 2, 128], F32),
                    swT=[A.get([128, 2, 128], BF16) for _ in range(2)], tmpA=A.get([128, 2, 132], F32), tot=A.get([128, 2, 132], F32),
                    den=A.get([128, 2], F32), den2=A.get([128, 2], F32), rden=A.get([128, 2], F32), hh=A.get([128, 2, 128], F32)))
            memset(Cb[:, :, NT, :], 0.0, [("Cb", 0), ("Cb", 1)])
            KVK = [("KVd", c) for c in range(NT)]
            def head_core(h):
                for d_ in range(2):
                    ri = d_ * 4 + h
                    order = list(range(NT)) if d_ == 0 else list(range(NT - 1, -1, -1))
                    for i, c in enumerate(order):
                        v = vsb[i % 2]
                        kv = ("vs", i % 2)
                        act(v[:, 0:129], lva[:, c, 0:129], AF.Copy, ["lva", "lva1", "cols"], [kv], scale=cols[:, c, 2, ri:ri + 1])
                        b = i % 2
                        mm(ps[b][:, 0:129], ktok[:, c, :], v[:, 0:129], True, True, ["ktok", kv], [("ps", b)])
                        if i == 0:
                            cp(KVd[:, c, 0:129], ps[b][:, 0:129], [("ps", b)], [("KVd", c)])
                        else:
                            cp_ = order[i - 1]
                            stt(KVd[:, c, 0:129], KVd[:, cp_, 0:129], decbc[:, ri, c:c + 1], ps[b][:, 0:129], ALU.mult, ALU.add,
                                [("KVd", cp_), ("ps", b), "decbc"], [("KVd", c)])
                    cp(Cb[:, d_, 0:NT, 0:129], KVd[:, :, 0:129], KVK, [("Cb", d_)], eng="act")
                acolv = lambda c, q: cols[:, c, q, h:h + 5:4]

                def chunk_stages(c, s_, par):
                    t_ = T[s_]
                    swT_ = t_["swT"][par]
                    kW = ("swT", s_, par)
                    pS, pB, pA = ps[2 + s_], ps[4 + s_], ps[6 + s_]
                    kS, kB, kA = ("ps", 2 + s_), ("ps", 4 + s_), ("ps", 6 + s_)
                    cs = slice(c * 128, (c + 1) * 128)
                    K = lambda n: (n, s_)
                    pBv = pB[:, 0:264].rearrange("p (d e) -> p d e", d=2)[:, :, 0:129]

                    def f0():
                        mm(pS[:, 0:128], cqk[:, 1, cs], cqk[:, 0, cs], True, True, [("cqk", 0), ("cqk", 1)], [kS])
                        for d_ in range(2):
                            for hl in range(2):
                                mm(pS[:, 128 + d_ * 128: 256 + d_ * 128], selb[:, d_ * 4 + h, :], Mhl[:, hl, cs], hl == 0, hl == 1, ["selb", "Mhl"], [kS])

                    def f1():
                        tt(t_["d2"], pS[:, 128:384].rearrange("p (d t) -> p d t", d=2), acolv(c, 0).unsqueeze(2).to_broadcast([128, 2, 128]),
                           ALU.subtract, [kS, "cols"], [K("d2")])

                    def f2():
                        act(t_["w"], t_["d2"], AF.Exp, [K("d2")], [K("w")], scale=-1.0)
                        tt(t_["stm"], pS[:, 0:128].unsqueeze(1).to_broadcast([128, 2, 128]), maskT[:, :, :], ALU.mult, [kS, "maskT"], [K("stm")])

                    def f3():
                        stt(swT_, t_["w"], 1.0, t_["stm"], ALU.min, ALU.mult, [K("w"), K("stm")], [kW])

                    def b0():
                        for d_ in range(2):
                            mm(pB[:, d_ * 132: d_ * 132 + 129], swT_[:, d_, :], lva[:, c, 0:129], True, True, [kW, "lva", "lva1"], [kB])
                        for d_ in range(2):
                            cprev = c - 1 if d_ == 0 else c + 1
                            slot = cprev if 0 <= cprev < NT else NT
                            mm(pA[:, d_ * 132: d_ * 132 + 129], cqk[:, 0, cs], Cb[:, d_, slot, 0:129], True, True, [("cqk", 0), ("Cb", d_)], [kA])

                    def b1():
                        for d_ in range(2):
                            act(t_["tmpA"][:, d_, 0:129], pA[:, d_ * 132: d_ * 132 + 129], AF.Copy, [kA, "cols"], [K("tmpA")], scale=cols[:, c, 3, d_ * 4 + h: d_ * 4 + h + 1])

                    def b2():
                        tt(t_["tot"][:, :, 0:129], t_["tmpA"][:, :, 0:129], pBv, ALU.add, [K("tmpA"), kB], [K("tot")])
                        stt(t_["den"], t_["tot"][:, :, 128], -1.0, t_["tot"][:, :, 128], ALU.mult, ALU.max, [K("tot")], [K("den")])
                        tt(t_["den2"], t_["den"], acolv(c, 1), ALU.max, [K("den"), "cols"], [K("den2")])
                        recip(t_["rden"], t_["den2"], [K("den2")], [K("rden")])

                    def b3():
                        for d_ in range(2):
                            act(t_["hh"][:, d_, :], t_["tot"][:, d_, 0:128], AF.Copy, [K("tot"), K("rden")], [K("hh")], scale=t_["rden"][:, d_:d_ + 1])
                        tt(hsum[:, c, :], t_["hh"][:, 0, :], t_["hh"][:, 1, :], ALU.add, [K("hh")], [("hsum", c)], eng="pool")
                    return [f0, f1, f2, f3], [b0, b1, b2, b3]

                NP_ = NT // 2
                stg = [(chunk_stages(i, 0, i % 2), chunk_stages(i + NP_, 1, i % 2)) for i in range(NP_)]

                def emit_part(i, part):
                    (fx, bx), (fy, by) = stg[i]
                    for gx, gy in zip((fx, bx)[part], (fy, by)[part]):
                        gx()
                        gy()

                emit_part(0, 0)
                for i in range(NP_):
                    if i + 1 < NP_:
                        emit_part(i + 1, 0)
                    emit_part(i, 1)
                if h == 0:
                    tap("hsum0", hsum, [128, NT, 128], [("hsum", c) for c in range(NT)])
            def finalize_parts(h):
                st_ = {}

                def f_norm():
                    for c in range(NT):
                        act(hjunk, hsum[:, c, :], AF.Square, [("hsum", c)], ["hjunk", "ssh"], accum=ssh[:, c:c + 1])
                    rsqrt_cols(rsh, ssh, 1.0 / 128, ["ssh"], "rsh", tha, thb)
                    tt(hn, hsum, rsh.unsqueeze(2).to_broadcast([128, NT, 128]), ALU.mult, [("hsum", c) for c in range(NT)] + ["rsh"], ["hn"] + KVK)
                    st_["wi"] = loadw(w_in_d[l][:, 3584 + h * 128: 3584 + (h + 1) * 128])

                def f_tg(tg):
                    wi = st_["wi"]
                    b = 2 + (tg % 2)
                    b2 = 4 + (tg % 2)
                    for k in range(8):
                        mm(ps[b2][:, :], wt[wi][:, k, :], xnT[:, k, tg * 512:(tg + 1) * 512], k == 0, k == 7, [("wt", wi)] + NK(TG(tg)), [("ps", b2)])
                    act(sgl, ps[b2][:, :], AF.Sigmoid, [("ps", b2)], ["sgl"])
                    pT = psb16[b][:, 0:512].rearrange("p (c t) -> p c t", c=4)
                    for i in range(4):
                        tr(pT[:, i, :], hn[:, tg * 4 + i, :], ident_b[:], ["hn", "ident_b"] + KVK, [("ps", b)])
                    stt(yc[:, h, tg * 512:(tg + 1) * 512], psb16[b][:, 0:512], colp[:, CP_LN + h: CP_LN + h + 1], sgl, ALU.mult, ALU.mult,
                        [("ps", b), "sgl", "colp"], [("y", 2, t) for t in TG(tg)])
                return [f_norm, lambda: f_tg(0), lambda: f_tg(1), lambda: f_tg(2), lambda: f_tg(3)]

            for h in range(4):
                head_core(h)
                fin = finalize_parts(h)
                tk = Ticker(setup_parts(h + 1) if h + 1 < 4 else [])
                tk.tick(4)
                for f_ in fin:
                    f_()
                    tk.tick(4)
                tk.flush()
            tap("ycT", yc, [128, 4, S], [("y", 2, t) for t in range(NT)], BF16)

        def stage_sgu(l):
            ya = ybuf[:, 0]
            A = Arena(BIG, [(16 * 1024, 32 * 1024), (YB, YB + AR)])
            vln = A.get([128, NT, 512], BF16)
            pbx = A.get([128, 512], F32)
            Eb = A.get([128, 4, 128], F32)
            tmp = [A.get([128, 512], F32) for _ in range(2)]
            junk = A.get([128, 512], BF16)
            s1 = A.get([128, NT], F32)
            s2 = A.get([128, NT], F32)
            mean = A.get([128, NT], F32)
            var = A.get([128, NT], F32)
            ta = A.get([128, NT], F32)
            tb = A.get([128, NT], F32)
            rstd = A.get([128, NT], F32)
            bsb = pbx
            P.dma("sp", pbx, pbc_d[l][:, PB_BSB:PB_BSB + 512], writes=["pbx"])
            P.dma("pool", wsT[:], sguw_d[l], writes=["wsT"])
            mm(ps[7][:, :], ones_b[:], wsT[:, :, :].rearrange("p g t -> p (g t)"), True, True, ["ones_b", "wsT"], [("ps", 7)])
            for g in range(4):
                stt(Eb[:, g, :], ps[7][:, g * 128:(g + 1) * 128], colp[:, CP_LB + g:CP_LB + g + 1], bsb[:, g * 128:(g + 1) * 128], ALU.mult, ALU.add,
                    [("ps", 7), "colp", "pbx"], ["Eb"])
            pbank = [0]

            def nb01():
                pbank[0] ^= 1
                return pbank[0]
            for c in range(4):
                wi = loadw(w_in_d[l][:, c * 128:(c + 1) * 128])
                for tg in range(4):
                    b = nb01()
                    for k in range(8):
                        mm(ps[b][:, :], wt[wi][:, k, :], xnT[:, k, tg * 512:(tg + 1) * 512], k == 0, k == 7, [("wt", wi)] + NK(TG(tg)), [("ps", b)])
                    act(ya[:, c, tg * 512:(tg + 1) * 512], ps[b][:, :], AF.Gelu_apprx_tanh, [("ps", b)], [("y", 0, t) for t in TG(tg)])
            wis = [loadw(w_in_d[l][:, 512 + c * 128: 512 + (c + 1) * 128]) for c in range(4)]
            for t in range(NT):
                b = 2 + nb01()
                for c in range(4):
                    for k in range(8):
                        mm(ps[b][:, c * 128:(c + 1) * 128], xnT[:, k, t * 128:(t + 1) * 128], wt[wis[c]][:, k, :], k == 0, k == 7, [("wt", wis[c])] + NK([t]), [("ps", b)])
                act(vln[:, t, :], ps[b][:, :], AF.Gelu_apprx_tanh, [("ps", b)], [("vln", t), "s1"], accum=s1[:, t:t + 1])
                act(junk, vln[:, t, :], AF.Square, [("vln", t)], ["sjunk", "s2"], accum=s2[:, t:t + 1])
            ts(mean, s1, 1.0 / 512, None, ALU.mult, None, ["s1"], ["mean"])
            tt(var, mean, mean, ALU.mult, ["mean"], ["var"])
            stt(var, s2, 1.0 / 512, var, ALU.mult, ALU.subtract, ["s2", "var"], ["var"])
            rsqrt_cols(rstd, var, 1.0, ["var"], "srstd", ta, tb)
            for t in range(NT):
                tm = tmp[t % 2]
                kt = ("stmp", t % 2)
                ts(vln[:, t, :], vln[:, t, :], mean[:, t:t + 1], rstd[:, t:t + 1], ALU.subtract, ALU.mult, [("vln", t), "mean", "srstd"], [("vln", t)])
                b = 4 + nb01()
                for g in range(4):
                    mm(ps[b][:, g * 128:(g + 1) * 128], vln[:, t, g * 128:(g + 1) * 128], wsT[:, g, :], True, True, [("vln", t), "wsT"], [("ps", b)])
                for g in range(4):
                    stt(tm[:, g * 128:(g + 1) * 128], ps[b][:, g * 128:(g + 1) * 128], colp[:, CP_LG + g:CP_LG + g + 1], Eb[:, g, :], ALU.mult, ALU.add,
                        [("ps", b), "colp", "Eb"], [kt])
                yav = ya[:, :, t * 128:(t + 1) * 128]
                tt(yav, tm.rearrange("p (g t) -> p g t", g=4), yav, ALU.mult, [kt, ("y", 0, t)], [("y", 0, t)])
            tap("yaT", ya, [128, 4, S], [("y", 0, t) for t in range(NT)], BF16)

        def stage_attn(l):
            yb = ybuf[:, 1]
            A = Arena(BIG, [(YB, YB + AR)])
            kT = A.get([128, 2, S], BF16)
            vatt = A.get([128, NT, 256], BF16)
            cosT = A.get([128, S], F32)
            sinT = A.get([128, S], F32)
            pmf = A.get([128, 128], F32)
            pmb = A.get([128, 128], BF16)
            amark = [list(r) for r in A.regions]
            xsq = [A.get([128, 512], BF16) for _ in range(2)]
            lr = [A.get([128, 512], F32) for _ in range(2)]
            xnb = [A.get([128, 512], BF16) for _ in range(2)]
            t1 = A.get([128, 512], F32)
            t2 = A.get([128, 512], F32)
            P.dma("sp", cosT, cos_d, writes=["cos"])
            P.dma("sp", sinT, sin_d, writes=["sin"])
            P.dma("sp", pmf, pm_d, writes=["pmf"])
            cp(pmb, pmf, ["pmf"], ["pmb"])
            qsteps = [(j, tg) for j in range(6) for tg in range(4)]
            wq = {}
            XB = [0, 1, 2]

            def q_X(n):
                j, tg = qsteps[n]
                if tg == 0:
                    wq[j] = loadw(w_in_d[l][:, 1024 + j * 128: 1024 + (j + 1) * 128])
                wi = wq[j]
                b = XB[n % 3]
                for k in range(8):
                    mm(ps[b][:, :], wt[wi][:, k, :], xnT[:, k, tg * 512:(tg + 1) * 512], k == 0, k == 7, [("wt", wi)] + NK(TG(tg)), [("ps", b)])

            def q_rest(n):
                j, tg = qsteps[n]
                s_ = n % 2
                bX, bS, bP = XB[n % 3], 3 + s_, 5 + s_
                sl = slice(tg * 512, (tg + 1) * 512)
                gcol = colp[:, CP_QN:CP_QN + 1] if j < 4 else colp[:, CP_KN:CP_KN + 1]
                act(xsq[s_], ps[bX][:, :], AF.Square, [("ps", bX)], [("xsq", s_)])
                mm(ps[bS][:, :], ones_b[:], xsq[s_], True, True, ["ones_b", ("xsq", s_)], [("ps", bS)])
                if n + 2 < len(qsteps):
                    q_X(n + 2)
                act(lr[s_], ps[bS][:, :], AF.Ln, [("ps", bS)], [("lr", s_)], bias=EPS, scale=1.0 / 128)
                act(lr[s_], lr[s_], AF.Exp, [("lr", s_)], [("lr", s_)], scale=-0.5)
                stt(xnb[s_], ps[bX][:, :], gcol, lr[s_], ALU.mult, ALU.mult, [("ps", bX), ("lr", s_), "colp"], [("xnb", s_)])
                mm(ps[bP][:, :], pmb, xnb[s_], True, True, ["pmb", ("xnb", s_)], [("ps", bP)])
                tt(t1, xnb[s_], cosT[:, sl], ALU.mult, [("xnb", s_), "cos"], ["t1"])
                tt(t2, ps[bP][:, :], sinT[:, sl], ALU.mult, [("ps", bP), "sin"], ["t2"])
                if j < 4:
                    tt(yb[:, j, sl], t1, t2, ALU.add, ["t1", "t2"], [("y", 1, t) for t in TG(tg)])
                else:
                    tt(kT[:, j - 4, sl], t1, t2, ALU.add, ["t1", "t2"], [("kT", j - 4)])

            q_X(0)
            q_X(1)
            for n in range(len(qsteps)):
                q_rest(n)
            wis = [loadw(w_in_d[l][:, 1792 + c * 128: 1792 + (c + 1) * 128]) for c in range(2)]
            for t in range(NT):
                b = 2 + (t % 2)
                for c in range(2):
                    for k in range(8):
                        mm(ps[b][:, c * 128:(c + 1) * 128], xnT[:, k, t * 128:(t + 1) * 128], wt[wis[c]][:, k, :], k == 0, k == 7, [("wt", wis[c])] + NK([t]), [("ps", b)])
                evac(vatt[:, t, :], ps[b][:, 0:256], [("ps", b)], ["vatt"])
            P.barrier()
            A.regions = amark
            PT = [A.get([128, 512], BF16) for _ in range(3)]
            rrec = A.get([128, 512], F32)
            tap("qT", yb, [128, 4, S], [("y", 1, t) for t in range(NT)], BF16)
            tap("kT", kT, [128, 2, S], [("kT", 0), ("kT", 1)], BF16)
            scale = 128.0 ** -0.5
            asteps = [(h, tg, kc) for h in range(4) for tg in range(4) for kc in range(NT)]
            STB = [0, 1, 6]

            def ST(n):
                h, tg, kc = asteps[n]
                g = h // 2
                b = STB[n % 3]
                mm(ps[b][:, :], kT[:, g, kc * 128:(kc + 1) * 128], yb[:, h, tg * 512:(tg + 1) * 512], True, True,
                   [("kT", g)] + [("y", 1, t) for t in TG(tg)], [("ps", b)])
            ST(0)
            ST(1)
            for n, (h, tg, kc) in enumerate(asteps):
                g = h // 2
                blk = n // NT
                pO, pR = 2 + (blk % 2), 4 + (blk % 2)
                if n + 2 < len(asteps):
                    ST(n + 2)
                b = STB[n % 3]
                pt = PT[n % 3]
                act(pt, ps[b][:, :], AF.Exp, [("ps", b)], [("PT", n % 3)], scale=scale)
                mm(ps[pO][:, :], vatt[:, kc, g * 128:(g + 1) * 128], pt, kc == 0, kc == NT - 1, ["vatt", ("PT", n % 3)], [("ps", pO)])
                mm(ps[pR][:, :], ones_b[:], pt, kc == 0, kc == NT - 1, ["ones_b", ("PT", n % 3)], [("ps", pR)])
                if kc == NT - 1:
                    qs = slice(tg * 512, (tg + 1) * 512)
                    recip(rrec, ps[pR][:, :], [("ps", pR)], ["rrec"])
                    tt(yb[:, h, qs], ps[pO][:, :], rrec, ALU.mult, [("ps", pO), "rrec"], [("y", 1, t) for t in TG(tg)])
            tap("ybT", yb, [128, 4, S], [("y", 1, t) for t in range(NT)], BF16)

        def stage_mix(l):
            A = Arena(BIG, [(YB, YB + AR)])
            mixT = A.get([128, 8, 1024], BF16)
            acc = A.get([128, 1024], F32)
            sg = [A.get([128, 512], F32) for _ in range(2)]
            tm = [A.get([128, 512], F32) for _ in range(2)]
            flip = [0]
            for tb_ in range(2):
                for dc in range(8):
                    for n in range(3):
                        wg = loadw(w_gate_d[l][:, n * 1024 + dc * 128: n * 1024 + (dc + 1) * 128])
                        wb = loadw(w_br_d[l][n][:, dc * 128:(dc + 1) * 128], nchunk=4)
                        for tgi in range(2):
                            tg = tb_ * 2 + tgi
                            f = flip[0]
                            flip[0] ^= 1
                            sl = slice(tg * 512, (tg + 1) * 512)
                            for k in range(8):
                                mm(ps[f][:, :], wt[wg][:, k, :], xnT[:, k, sl], k == 0, k == 7, [("wt", wg)] + NK(TG(tg)), [("ps", f)])
                            act(sg[f], ps[f][:, :], AF.Sigmoid, [("ps", f), "colp"], [("sg", f)], bias=colp[:, CP_BG + n * 8 + dc: CP_BG + n * 8 + dc + 1])
                            for k in range(4):
                                mm(ps[2 + f][:, :], wt[wb][:, k, :], ybuf[:, n, k, sl], k == 0, k == 3, [("wt", wb)] + [("y", n, t) for t in TG(tg)], [("ps", 2 + f)])
                            asl = acc[:, tgi * 512:(tgi + 1) * 512]
                            if n == 0:
                                tt(asl, sg[f], ps[2 + f][:, :], ALU.mult, [("sg", f), ("ps", 2 + f)], [("acc", tgi)])
                            else:
                                tt(tm[f], sg[f], ps[2 + f][:, :], ALU.mult, [("sg", f), ("ps", 2 + f)], [("tm", f)])
                                if n == 1:
                                    tt(asl, asl, tm[f], ALU.add, [("acc", tgi), ("tm", f)], [("acc", tgi)])
                                else:
                                    tt(mixT[:, dc, tgi * 512:(tgi + 1) * 512], asl, tm[f], ALU.add, [("acc", tgi), ("tm", f)], [("mixT", tgi)])
                if tb_ == 0:
                    tap("mixT0", mixT, [128, 8, 1024], [("mixT", 0), ("mixT", 1)], BF16)
                for cg in range(8):
                    wo = loadw(w_out_d[l][:, cg * 128:(cg + 1) * 128])
                    for t4 in range(2):
                        b = 4 + (t4 % 2) + 2 * (cg % 2)
                        pv = ps[b].rearrange("p (c t) -> p c t", c=4)
                        for i in range(4):
                            tl = t4 * 4 + i
                            for k in range(8):
                                mm(pv[:, i, :], mixT[:, k, tl * 128:(tl + 1) * 128], wt[wo][:, k, :], k == 0, k == 7, [("wt", wo), ("mixT", tl // 4)], [("ps", b)])
                        t0 = tb_ * 8 + t4 * 4
                        xv_ = x_sb[:, t0:t0 + 4, cg * 128:(cg + 1) * 128]
                        tt(xv_, xv_, pv, ALU.add, XK(range(t0, t0 + 4)) + [("ps", b)], XK(range(t0, t0 + 4)))

        ov = out_d.rearrange("(t p) d -> p t d", p=128)

        def store_out(q):
            P.dma("sp", ov[:, 4 * q:4 * q + 4, :], x_sb[:, 4 * q:4 * q + 4, :], reads=XK(range(4 * q, 4 * q + 4)), is_output=True)

        def stage_ffn(l):
            A = Arena(BIG, [(0, YB + AR)])
            actT = A.get([128, 22, 1024], BF16)
            sb = [A.get([128, 512], F32) for _ in range(2)]
            flip = [0]
            for tb_ in range(2):
                for j in range(22):
                    w1 = loadw(w_f1_d[l][:, j * 128:(j + 1) * 128])
                    w2 = loadw(w_f1_d[l][:, FH + j * 128: FH + (j + 1) * 128])
                    for tgi in range(2):
                        tg = tb_ * 2 + tgi
                        f = flip[0]
                        flip[0] ^= 1
                        sl = slice(tg * 512, (tg + 1) * 512)
                        for k in range(8):
                            mm(ps[f][:, :], wt[w1][:, k, :], xnT[:, k, sl], k == 0, k == 7, [("wt", w1)] + NK(TG(tg)), [("ps", f)])
                        for k in range(8):
                            mm(ps[2 + f][:, :], wt[w2][:, k, :], xnT[:, k, sl], k == 0, k == 7, [("wt", w2)] + NK(TG(tg)), [("ps", 2 + f)])
                        act(sb[f], ps[f][:, :], AF.Silu, [("ps", f)], [("fs", f)])
                        tt(actT[:, j, tgi * 512:(tgi + 1) * 512], sb[f], ps[2 + f][:, :], ALU.mult, [("fs", f), ("ps", 2 + f)], [("actT", tgi)])
                for cg in range(8):
                    wos = [loadw(w_f2_d[l][r0 * 128:(r0 + n_) * 128, cg * 128:(cg + 1) * 128], nchunk=n_) for (r0, n_) in ((0, 8), (8, 8), (16, 6))]
                    for t4 in range(2):
                        b = 4 + (t4 % 2) + 2 * (cg % 2)
                        pv = ps[b].rearrange("p (c t) -> p c t", c=4)
                        for i in range(4):
                            tl = t4 * 4 + i
                            for j in range(22):
                                wi = wos[j // 8]
                                mm(pv[:, i, :], actT[:, j, tl * 128:(tl + 1) * 128], wt[wi][:, j % 8, :], j == 0, j == 21, [("wt", wi), ("actT", tl // 4)], [("ps", b)])
                        t0 = tb_ * 8 + t4 * 4
                        xv_ = x_sb[:, t0:t0 + 4, cg * 128:(cg + 1) * 128]
                        tt(xv_, xv_, pv, ALU.add, XK(range(t0, t0 + 4)) + [("ps", b)], XK(range(t0, t0 + 4)))
                if l == depth - 1 and not stop_after:
                    store_out(2 * tb_)
                    store_out(2 * tb_ + 1)

        def run():
            for l in range(depth):
                P.dma("sp", colp[:], colp_d[l], writes=["colp"])
                for name, fn in (("norm", lambda: stage_norm(l, CP_GM)), ("mlstm", lambda: stage_mlstm(l)), ("sgu", lambda: stage_sgu(l)),
                                 ("attn", lambda: stage_attn(l)), ("mix", lambda: stage_mix(l)), ("norm2", lambda: stage_norm(l, CP_GF)),
                                 ("ffn", lambda: stage_ffn(l))):
                    if name in dbg.get("skip", ()):
                        continue
                    fn()
                    P.barrier()
                    if name == "norm":
                        tap("xnT", xnT[:], [128, 8, S], NK(range(NT)), BF16)
                    if stop_after == (l, name):
                        return
        run()
        if stop_after or "ffn" in dbg.get("skip", ()):
            for q in range(4):
                store_out(q)
        P.finish()
        P.emit()
    return nc, tap_d


def host_consts():
    rows = S // 64
    row = np.repeat(np.arange(rows, dtype=np.float32), 64)
    col = np.tile(np.arange(64, dtype=np.float32), rows)
    freqs = (np.float32(10000.0) ** (-np.arange(32, dtype=np.float32) * np.float32(2.0) / np.float32(64))).astype(np.float32)
    ang = np.concatenate([row[:, None] * freqs[None], col[:, None] * freqs[None]], axis=-1).astype(np.float32)
    cos = np.repeat(np.cos(ang).astype(np.float32).T, 2, axis=0)
    sin = np.repeat(np.sin(ang).astype(np.float32).T, 2, axis=0)
    pm = np.zeros((128, 128), np.float32)
    for j in range(64):
        pm[2 * j + 1, 2 * j] = -1.0
        pm[2 * j, 2 * j + 1] = 1.0
    sc = np.float32(128.0 ** -0.5)
    s_idx = np.arange(128)[:, None]
    t_idx = np.arange(128)[None, :]
    mask = np.zeros((128, 2, 128), np.float32)
    mask[:, 0, :] = np.where(s_idx <= t_idx, sc, 0.0)
    mask[:, 1, :] = np.where(s_idx >= t_idx, sc, 0.0)
    sel = np.zeros((64, 8, 128), np.float32)
    for ri, r in enumerate(ROWS):
        sel[r, ri, :] = 1.0
    return dict(cos_t=np.ascontiguousarray(cos), sin_t=np.ascontiguousarray(sin), mask_t=mask, sel_t=sel,
                ident_f=np.eye(128, dtype=np.float32), pm_t=pm)


def host_layout(inputs):
    f = lambda k: np.asarray(inputs[k], dtype=np.float32)
    pbc = np.zeros((DEPTH, 128, NPB), np.float32)
    colp = np.zeros((DEPTH, 128, NCP), np.float32)
    for l in range(DEPTH):
        rowv = np.concatenate([f("norm_mix")[l], f("norm_ffn")[l], f("sgu_ln_g")[l], f("sgu_ln_b")[l],
                               f("sgu_b")[l].reshape(-1), f("q_norm")[l], f("k_norm")[l]])
        pbc[l] = np.broadcast_to(rowv[None, :], (128, NPB))
        colp[l, :, CP_BG:CP_BG + 24] = f("b_gate")[l].reshape(24, 128).T
        colp[l, :, CP_CB:CP_CB + 8] = f("conv_b")[l].reshape(8, 128).T
        colp[l, :, CP_CW:CP_CW + 40] = f("conv_w")[l].reshape(5, 8, 128).transpose(2, 0, 1).reshape(128, 40)
        colp[l, :, CP_LN:CP_LN + 4] = f("lstm_norm")[l].reshape(4, 128).T
        colp[l, :, CP_GM:CP_GM + 8] = f("norm_mix")[l].reshape(8, 128).T
        colp[l, :, CP_GF:CP_GF + 8] = f("norm_ffn")[l].reshape(8, 128).T
        colp[l, :, CP_QN] = f("q_norm")[l]
        colp[l, :, CP_KN] = f("k_norm")[l]
        colp[l, :, CP_LG:CP_LG + 4] = f("sgu_ln_g")[l].reshape(4, 128).T
        colp[l, :, CP_LB:CP_LB + 4] = f("sgu_ln_b")[l].reshape(4, 128).T
        for d_ in range(2):
            for h in range(4):
                colp[l, d_ * 32 + h, CP_IGB] = f("igate_b")[l, d_, h]
                colp[l, d_ * 32 + h, CP_FGB] = f("fgate_b")[l, d_, h]
    sgu_wT = np.ascontiguousarray(f("sgu_w").transpose(0, 3, 1, 2))
    shared = dict(w_in=f("w_in"), w_gate=f("w_gate"), w_branch=f("w_branch"), w_out=f("w_out"),
                  w_ffn_in=f("w_ffn_in"), w_ffn_out=f("w_ffn_out"), sgu_wT=sgu_wT, pbc=pbc, colp=colp)
    shared.update(host_consts())
    return shared


_CACHE = {}


def kernel(**inputs):
    shared = host_layout(inputs)
    x = np.asarray(inputs["x"], dtype=np.float32)
    if "nc" not in _CACHE:
        _CACHE["nc"] = build_program()[0]
    nc = _CACHE["nc"]
    in_maps = [dict(shared, x=np.ascontiguousarray(x[b])) for b in range(NB)]
    res = run_bass_kernel_spmd(nc, in_maps, core_ids=list(range(NB)))
    return np.stack([np.asarray(r["out"], dtype=np.float32) for r in res.results], axis=0)
```
